# Optimizing a Trainium2 kernel written in Bass

```python
import jax, jax.numpy as jnp
from jax import lax
import numpy as np

D_MODEL = 2048
BATCH = 2
SEQ = 4096
DEPTH = 1

GDN_HEADS = 8
GDN_DK = 128
GDN_DV = 128
GDN_KEY = GDN_HEADS * GDN_DK
GDN_VAL = GDN_HEADS * GDN_DV
MLSTM_HEADS = 8
MLSTM_DH = 128
MLSTM_W = MLSTM_HEADS * MLSTM_DH
N_DIR = 2
N_BRANCH = 2
CONV_WIDTH = 5
CHUNK = 64
NORM_EPS = 1e-6
IN_SIZES = (
    2 * GDN_KEY + GDN_VAL,
    GDN_VAL,
    N_DIR * GDN_HEADS,
    N_DIR * GDN_HEADS,
    2 * MLSTM_W,
    MLSTM_W,
    MLSTM_W,
    MLSTM_W,
    N_DIR * MLSTM_HEADS,
    N_DIR * MLSTM_HEADS,
    N_BRANCH * D_MODEL,
)
IN_COLS = sum(IN_SIZES)

kernel_name = "bidir_gdn_mlstm_gated_hybrid"


def rmsnorm(x, w):
    xf = x.astype(jnp.float32)
    y = xf * lax.rsqrt(jnp.mean(xf * xf, axis=-1, keepdims=True) + NORM_EPS)
    return (y * w.astype(jnp.float32)).astype(x.dtype)


def l2norm(x):
    return x * lax.rsqrt(jnp.sum(x * x, axis=-1, keepdims=True) + NORM_EPS)


def centred_depthwise_conv(x, w):
    pad = (w.shape[0] - 1) // 2
    return lax.conv_general_dilated(
        x, w[:, None, :].astype(x.dtype), window_strides=(1,), padding=[(pad, pad)],
        dimension_numbers=("NWC", "WIO", "NWC"), feature_group_count=x.shape[-1])


def to_heads(t, n_heads):
    b, s, c = t.shape
    return t.reshape(b, s, n_heads, c // n_heads).transpose(0, 2, 1, 3)


def gate_heads(t, n_heads):
    b, s, _ = t.shape
    return t.reshape(b, s, N_DIR, n_heads).transpose(2, 0, 3, 1)


def to_chunks(t):
    b, h, s = t.shape[:3]
    return t.reshape((b, h, s // CHUNK, CHUNK) + t.shape[3:])


def gated_delta_chunked(q, k, v, g, beta):
    b_, h_, s_, dk = q.shape
    dv = v.shape[-1]
    q, k, v = to_chunks(q), to_chunks(k), to_chunks(v)
    g, beta = to_chunks(g), to_chunks(beta)
    gc = jnp.cumsum(g, axis=-1)
    causal = jnp.tril(jnp.ones((CHUNK, CHUNK), dtype=bool))
    strict = jnp.tril(jnp.ones((CHUNK, CHUNK), dtype=bool), k=-1)
    decay = jnp.exp(jnp.where(causal, gc[..., :, None] - gc[..., None, :], -jnp.inf))
    kb = k * beta[..., None]
    lower = jnp.where(strict, jnp.einsum("bhnik,bhnjk->bhnij", kb, k) * decay, 0.0)
    a_mat = lower + jnp.eye(CHUNK, dtype=q.dtype)
    rhs = jnp.concatenate([v * beta[..., None], kb * jnp.exp(gc)[..., None]], axis=-1)
    sol = lax.linalg.triangular_solve(a_mat, rhs, left_side=True, lower=True, unit_diagonal=True)
    u, w = sol[..., :dv], sol[..., dv:]
    attn = jnp.einsum("bhnik,bhnjk->bhnij", q, k) * decay
    g_last = gc[..., -1]
    k_dec = k * jnp.exp(g_last[..., None] - gc)[..., None]

    def step(state, inp):
        u_c, w_c, kd_c, gl_c = inp
        v_new = u_c - jnp.einsum("bhlk,bhkv->bhlv", w_c, state)
        state_next = state * jnp.exp(gl_c)[..., None, None] + jnp.einsum("bhlk,bhlv->bhkv", kd_c, v_new)
        return state_next, (state, v_new)

    init = jnp.zeros((b_, h_, dk, dv), q.dtype)
    xs = tuple(jnp.moveaxis(t, 2, 0) for t in (u, w, k_dec, g_last))
    _, (s_all, v_new) = lax.scan(step, init, xs)
    s_all = jnp.moveaxis(s_all, 0, 2)
    v_new = jnp.moveaxis(v_new, 0, 2)
    out = (jnp.einsum("bhnlk,bhnkv->bhnlv", q * jnp.exp(gc)[..., None], s_all)
           + jnp.einsum("bhnij,bhnjv->bhniv", attn, v_new))
    return out.reshape(b_, h_, s_, dv)


def mlstm_chunked(q, k, v, i_pre, f_pre):
    b_, h_, s_, dk = q.shape
    dv = v.shape[-1]
    q, k, v = to_chunks(q), to_chunks(k), to_chunks(v)
    ig = to_chunks(i_pre)
    bcum = jnp.cumsum(jax.nn.log_sigmoid(to_chunks(f_pre)), axis=-1)
    causal = jnp.tril(jnp.ones((CHUNK, CHUNK), dtype=bool))
    d_log = jnp.where(causal, bcum[..., :, None] - bcum[..., None, :] + ig[..., None, :], -jnp.inf)
    m_intra = jnp.max(d_log, axis=-1)
    a_end = bcum[..., -1:] - bcum + ig
    m_loc = jnp.max(a_end, axis=-1)
    k_end = k * jnp.exp(a_end - m_loc[..., None])[..., None]
    d_c = jnp.einsum("bhnlk,bhnlv->bhnkv", k_end, v)
    d_n = jnp.sum(k_end, axis=-2)
    b_last = bcum[..., -1]

    def step(carry, inp):
        c_st, n_st, m_st = carry
        dc_c, dn_c, ml_c, bl_c = inp
        m_new = jnp.maximum(bl_c + m_st, ml_c)
        s_old = jnp.exp(bl_c + m_st - m_new)
        s_loc = jnp.exp(ml_c - m_new)
        c_next = c_st * s_old[..., None, None] + dc_c * s_loc[..., None, None]
        n_next = n_st * s_old[..., None] + dn_c * s_loc[..., None]
        return (c_next, n_next, m_new), (c_st, n_st, m_st)

    init = (jnp.zeros((b_, h_, dk, dv), q.dtype), jnp.zeros((b_, h_, dk), q.dtype),
            jnp.zeros((b_, h_), q.dtype))
    xs = tuple(jnp.moveaxis(t, 2, 0) for t in (d_c, d_n, m_loc, b_last))
    _, (c_all, n_all, m_all) = lax.scan(step, init, xs)
    c_all = jnp.moveaxis(c_all, 0, 2)
    n_all = jnp.moveaxis(n_all, 0, 2)
    m_all = jnp.moveaxis(m_all, 0, 2)
    m_inter = bcum + m_all[..., None]
    m_t = jnp.maximum(m_inter, m_intra)
    w_inter = jnp.exp(m_inter - m_t)
    scores = jnp.einsum("bhnik,bhnjk->bhnij", q, k) * jnp.exp(d_log - m_t[..., None])
    num = (w_inter[..., None] * jnp.einsum("bhnlk,bhnkv->bhnlv", q, c_all)
           + jnp.einsum("bhnij,bhnjv->bhniv", scores, v))
    den = w_inter * jnp.einsum("bhnlk,bhnk->bhnl", q, n_all) + jnp.sum(scores, axis=-1)
    h = num / jnp.maximum(jnp.abs(den), jnp.exp(-m_t))[..., None]
    return h.reshape(b_, h_, s_, dv)


def flip_t(t, axis):
    return jnp.flip(t, axis=axis)


def gdn_branch(qkv, z, beta_pre, a_pre, conv_w, a_log, dt_bias, norm_w):
    f32 = jnp.float32
    qkv = jax.nn.silu(centred_depthwise_conv(qkv, conv_w)).astype(f32)
    q = l2norm(to_heads(qkv[..., :GDN_KEY], GDN_HEADS)) * (GDN_DK ** -0.5)
    k = l2norm(to_heads(qkv[..., GDN_KEY:2 * GDN_KEY], GDN_HEADS))
    v = to_heads(qkv[..., 2 * GDN_KEY:], GDN_HEADS)
    beta = jax.nn.sigmoid(gate_heads(beta_pre.astype(f32), GDN_HEADS))
    g = (-jnp.exp(a_log.astype(f32))[:, None, :, None]
         * jax.nn.softplus(gate_heads(a_pre.astype(f32), GDN_HEADS) + dt_bias.astype(f32)[:, None, :, None]))
    o_fwd = gated_delta_chunked(q, k, v, g[0], beta[0])
    o_bwd = flip_t(gated_delta_chunked(flip_t(q, 2), flip_t(k, 2), flip_t(v, 2),
                                       flip_t(g[1], 2), flip_t(beta[1], 2)), 2)
    o = (o_fwd + o_bwd).transpose(0, 2, 1, 3)
    o = o * lax.rsqrt(jnp.mean(o * o, axis=-1, keepdims=True) + NORM_EPS) * norm_w.astype(f32)
    o = o.reshape(o.shape[0], o.shape[1], GDN_VAL) * jax.nn.silu(z.astype(f32))
    return o.astype(z.dtype)


def mlstm_branch(qk, v, o_pre, z, i_pre, f_pre, conv_w, i_bias, f_bias, norm_w):
    f32 = jnp.float32
    qk = jax.nn.silu(centred_depthwise_conv(qk, conv_w)).astype(f32)
    q = to_heads(qk[..., :MLSTM_W], MLSTM_HEADS)
    k = to_heads(qk[..., MLSTM_W:], MLSTM_HEADS) * (MLSTM_DH ** -0.5)
    v = to_heads(v.astype(f32), MLSTM_HEADS)
    ig = gate_heads(i_pre.astype(f32), MLSTM_HEADS) + i_bias.astype(f32)[:, None, :, None]
    fg = gate_heads(f_pre.astype(f32), MLSTM_HEADS) + f_bias.astype(f32)[:, None, :, None]
    h_fwd = mlstm_chunked(q, k, v, ig[0], fg[0])
    h_bwd = flip_t(mlstm_chunked(flip_t(q, 2), flip_t(k, 2), flip_t(v, 2),
                                 flip_t(ig[1], 2), flip_t(fg[1], 2)), 2)
    h = (h_fwd + h_bwd).transpose(0, 2, 1, 3)
    mu = jnp.mean(h, axis=-1, keepdims=True)
    hc = h - mu
    h = hc * lax.rsqrt(jnp.mean(hc * hc, axis=-1, keepdims=True) + NORM_EPS)
    h = h.reshape(h.shape[0], h.shape[1], MLSTM_W) * norm_w.astype(f32)
    h = h * jax.nn.sigmoid(o_pre.astype(f32)) * jax.nn.silu(z.astype(f32))
    return h.astype(z.dtype)


def setup_inputs(seed: int = 0) -> dict:
    key = jax.random.key(seed)
    ks = jax.random.split(key, 20)
    f32 = jnp.float32
    nrm = lambda k, shape, scale: jax.random.normal(k, shape, f32) * scale
    x = jax.random.normal(ks[0], (BATCH, SEQ, D_MODEL), f32)
    w_in = nrm(ks[1], (DEPTH, D_MODEL, IN_COLS), D_MODEL ** -0.5)
    conv_gdn = nrm(ks[2], (DEPTH, CONV_WIDTH, 2 * GDN_KEY + GDN_VAL), CONV_WIDTH ** -0.5)
    gdn_a_log = jnp.log(jax.random.uniform(ks[3], (DEPTH, N_DIR, GDN_HEADS), f32, 1.0, 16.0))
    dt = jnp.exp(jax.random.uniform(ks[4], (DEPTH, N_DIR, GDN_HEADS), f32, np.log(1e-3), np.log(1e-1)))
    gdn_dt_bias = dt + jnp.log(-jnp.expm1(-dt))
    gdn_norm_w = 1.0 + nrm(ks[5], (DEPTH, GDN_DV), 0.02)
    conv_mlstm = nrm(ks[6], (DEPTH, CONV_WIDTH, 2 * MLSTM_W), CONV_WIDTH ** -0.5)
    mlstm_i_bias = nrm(ks[7], (DEPTH, N_DIR, MLSTM_HEADS), 0.1)
    mlstm_f_bias = (jnp.broadcast_to(jnp.linspace(3.0, 6.0, MLSTM_HEADS, dtype=f32), (DEPTH, N_DIR, MLSTM_HEADS))
                    + nrm(ks[8], (DEPTH, N_DIR, MLSTM_HEADS), 0.1))
    mlstm_norm_w = 1.0 + nrm(ks[9], (DEPTH, MLSTM_W), 0.02)
    gate_bias = nrm(ks[10], (DEPTH, N_BRANCH * D_MODEL), 0.02)
    w_branch_gdn = nrm(ks[11], (DEPTH, GDN_VAL, D_MODEL), GDN_VAL ** -0.5)
    w_branch_mlstm = nrm(ks[12], (DEPTH, MLSTM_W, D_MODEL), MLSTM_W ** -0.5)
    w_out = nrm(ks[13], (DEPTH, D_MODEL, D_MODEL), D_MODEL ** -0.5)
    norm_w = 1.0 + nrm(ks[14], (DEPTH, D_MODEL), 0.02)
    final_norm_w = 1.0 + nrm(ks[15], (D_MODEL,), 0.02)
    return {"x": x, "w_in": w_in, "conv_gdn": conv_gdn, "gdn_a_log": gdn_a_log,
            "gdn_dt_bias": gdn_dt_bias, "gdn_norm_w": gdn_norm_w, "conv_mlstm": conv_mlstm,
            "mlstm_i_bias": mlstm_i_bias, "mlstm_f_bias": mlstm_f_bias, "mlstm_norm_w": mlstm_norm_w,
            "gate_bias": gate_bias, "w_branch_gdn": w_branch_gdn, "w_branch_mlstm": w_branch_mlstm,
            "w_out": w_out, "norm_w": norm_w, "final_norm_w": final_norm_w}


def reference(x, w_in, conv_gdn, gdn_a_log, gdn_dt_bias, gdn_norm_w, conv_mlstm, mlstm_i_bias,
              mlstm_f_bias, mlstm_norm_w, gate_bias, w_branch_gdn, w_branch_mlstm, w_out, norm_w,
              final_norm_w):
    split_idx = [int(i) for i in np.cumsum(IN_SIZES)[:-1]]
    b_, s_, _ = x.shape
    for layer in range(DEPTH):
        n = rmsnorm(x, norm_w[layer])
        proj = jnp.einsum("btd,dc->btc", n, w_in[layer])
        (g_qkv, g_z, g_beta, g_a, m_qk, m_v, m_o, m_z, m_i, m_f, gates) = jnp.split(proj, split_idx, axis=-1)
        y_a = gdn_branch(g_qkv, g_z, g_beta, g_a, conv_gdn[layer], gdn_a_log[layer],
                         gdn_dt_bias[layer], gdn_norm_w[layer])
        y_b = mlstm_branch(m_qk, m_v, m_o, m_z, m_i, m_f, conv_mlstm[layer], mlstm_i_bias[layer],
                           mlstm_f_bias[layer], mlstm_norm_w[layer])
        gates = jax.nn.sigmoid(gates + gate_bias[layer]).reshape(b_, s_, N_BRANCH, D_MODEL)
        merged = (gates[:, :, 0] * jnp.einsum("btc,cd->btd", y_a, w_branch_gdn[layer])
                  + gates[:, :, 1] * jnp.einsum("btc,cd->btd", y_b, w_branch_mlstm[layer]))
        x = x + jnp.einsum("btd,de->bte", merged, w_out[layer])
    return rmsnorm(x, final_norm_w)
```

```python
import os
import numpy as np
import concourse.bass as bass
import concourse.mybir as mybir
from concourse.bass_utils import run_bass_kernel_spmd

F32 = mybir.dt.float32
BF16 = mybir.dt.bfloat16
AF = mybir.ActivationFunctionType
ALU = mybir.AluOpType
AX = mybir.AxisListType

N_CORES = 8
NORM_EPS = 1e-6
NEG = -1.0e30


class Cfg:
    def __init__(self, D=2048, H=8, SEG=1024):
        self.D, self.H, self.SEG = D, H, SEG
        self.KC = D // 128
        self.GK = H * 128
        self.NP = SEG // 128
        self.NCH = SEG // 64
        self.TB = min(512, SEG)
        self.NTB = SEG // self.TB
        self.HG = min(4, H)
        self.NG = H // self.HG
        self.WB = self.HG * 128
        GK = self.GK
        sizes = [3 * GK, GK, 2 * H, 2 * H, 2 * GK, GK, GK, GK, 2 * H, 2 * H, 2 * D]
        offs = np.concatenate([[0], np.cumsum(sizes)]).astype(int)
        (self.o_gqkv, self.o_gz, self.o_gbeta, self.o_ga, self.o_mqk, self.o_mv, self.o_mo,
         self.o_mz, self.o_mi, self.o_mf, self.o_gates) = [int(v) for v in offs[:-1]]
        self.IN_COLS = int(offs[-1])
        c = 0
        self.c_ident = c; c += 128
        self.c_J = c; c += 128
        self.c_tri = c; c += 128
        self.c_tris = c; c += 128
        self.c_ones = c; c += 128
        self.c_sel = c; c += H * 128
        self.c_idH = c; c += H
        self.c_mbs = c; c += 128
        self.c_mbi = c; c += 128
        self.c_my1 = c; c += 72
        self.c_mc1 = c; c += 72
        self.c_my2 = c; c += 72
        self.c_mc2 = c; c += 72
        self.NCST = c


class T:
    __slots__ = ("h", "w", "r", "name", "x")

    def __init__(self, h, name="", x=False):
        self.h = h
        self.w = None
        self.r = {}
        self.name = name
        self.x = x

    def __getitem__(self, idx):
        return self.h[idx]


class K:
    SAME_ENGINE_SYNC = ("act", "dve", "pool")
    EPOCH = 16000

    def __init__(self, nc):
        self.nc = nc
        self.eng = {"pe": nc.tensor, "act": nc.scalar, "dve": nc.vector,
                    "pool": nc.gpsimd, "sp": nc.sync}
        self.semh = {}
        self.cnt = {}
        self.epoch = {}
        self.waited = {}
        self.nops = {e: 0 for e in self.eng}
        self.marks = []

    def _cur(self, base):
        ep = self.epoch.get(base, 0)
        key = "%s#%d" % (base, ep)
        if key not in self.semh:
            self.semh[key] = self.nc.alloc_semaphore(name="s_" + key.replace("#", "_"))
            self.cnt[key] = 0
        elif self.cnt[key] >= self.EPOCH:
            self.epoch[base] = ep + 1
            return self._cur(base)
        return key

    @staticmethod
    def _need(needs, rec):
        if rec is None:
            return
        k, v = rec
        if needs.get(k, 0) < v:
            needs[k] = v

    def _waits(self, e, reads, writes):
        needs = {}
        for t in reads:
            self._need(needs, t.w)
        for t in writes:
            self._need(needs, t.w)
            for k, v in t.r.items():
                self._need(needs, (k, v))
        for k, v in needs.items():
            if k.split("#")[0] == e and e not in self.SAME_ENGINE_SYNC:
                continue
            if self.waited.get((e, k), 0) >= v:
                continue
            if k.startswith("d_") and v != self.cnt[k]:
                raise RuntimeError("ambiguous DMA wait on %s: %d of %d issued" % (k, v, self.cnt[k]))
            self.eng[e].wait_ge(self.semh[k], v)
            self.waited[(e, k)] = v

    def op(self, e, fn, reads=(), writes=()):
        self.total = getattr(self, "total", 0) + 1
        if self.total > int(os.environ.get("MK_MAXOPS", "1000000000")):
            return None
        xr = [t for t in reads if t.x]
        if xr:
            reads = [t for t in reads if not t.x]
            writes = list(writes) + xr
        if os.environ.get("MK_GLOCK") and any(t.x and (os.environ["MK_GLOCK"] in ("1", t.name)) for t in writes):
            if not hasattr(self, "glock"):
                self.glock = T(None, "glock")
            writes = list(writes) + [self.glock]
        self._waits(e, reads, writes)
        key = self._cur(e)
        ins = fn(self.eng[e])
        self.cnt[key] += 1
        self.nops[e] += 1
        ins.then_inc(self.semh[key], 1)
        c = self.cnt[key]
        for t in reads:
            if t.r.get(key, 0) < c:
                t.r[key] = c
        for t in writes:
            t.w = (key, c)
            t.r = {}
        return ins

    def dma(self, q, name, out_t, out_ap, in_t, in_ap, **kw):
        if getattr(self, "total", 0) > int(os.environ.get("MK_MAXOPS", "1000000000")):
            return None
        self._waits(q, [in_t], [out_t])
        key = self._cur("d_" + name)
        ins = self.eng[q].dma_start(out=out_ap, in_=in_ap, **kw)
        ins.then_inc(self.semh[key], 16)
        self.cnt[key] += 16
        c = self.cnt[key]
        in_t.r[key] = c
        out_t.w = (key, c)
        out_t.r = {}
        return ins

    def wait_all(self, e, tiles):
        self._waits(e, tiles, [])

    def mark(self, label):
        self.marks.append((label, dict(self.nops)))

    def barrier(self):
        allkeys = [(k, v) for k, v in self.cnt.items() if v > 0]
        for e in ("pe", "act", "dve", "pool", "sp"):
            for k, v in allkeys:
                if k.split("#")[0] == e and e not in self.SAME_ENGINE_SYNC:
                    continue
                if self.waited.get((e, k), 0) >= v:
                    continue
                self.eng[e].wait_ge(self.semh[k], v)
                self.waited[(e, k)] = v


class Arena:
    def __init__(self, nc):
        self.nc = nc
        total = nc.SBUF_PARTITION_SIZE_BYTES
        self.base = ((total - nc.sbuf_bytes_remaining + 63) // 64) * 64
        self.limit = total - 256
        self.ptr = self.base
        self.n = 0

    def alloc(self, shape, dtype, name="t"):
        nb = 4 if dtype == F32 else 2
        sz = nb
        for s in shape[1:]:
            sz *= s
        sz = ((sz + 63) // 64) * 64
        if self.ptr + sz > self.limit:
            raise RuntimeError("SBUF arena overflow at %s: %d + %d > %d" % (name, self.ptr, sz, self.limit))
        self.n += 1
        h = self.nc.alloc_sbuf_tensor_at("%s_%d" % (name, self.n), list(shape), dtype, offset=self.ptr)
        self.ptr += sz
        return T(h, name)

    def mark(self):
        return self.ptr

    def release(self, mark):
        self.ptr = mark


def build_nc(cfg):
    D, H, SEG, KC, GK, NP, NCH = cfg.D, cfg.H, cfg.SEG, cfg.KC, cfg.GK, cfg.NP, cfg.NCH
    TB, NTB, HG, NG, WB = cfg.TB, cfg.NTB, cfg.HG, cfg.NG, cfg.WB
    H4 = 4 * H
    nc = bass.Bass("TRN2", target_bir_lowering=False)
    k = K(nc)
    A = Arena(nc)

    def din(name, shape):
        return T(nc.dram_tensor(name, list(shape), F32, kind="ExternalInput").ap(), name)

    xs_d = din("xs", [5, SEG, D])
    xh_d = din("xh", [32, D])
    win_d = din("w_in", [D, cfg.IN_COLS])
    wg_d = din("wg", [5, D, H4])
    gp_d = din("gp", [5, H4, 8])
    cvg_d = din("cvg", [5, 128, 3 * H, 5])
    cvm_d = din("cvm", [5, 128, 2 * H, 5])
    flags_d = din("flags", [128, 8])
    nwr_d = din("nwr", [128, D])
    gnw_d = din("gnw", [128, GK])
    mnw_d = din("mnw", [128, GK])
    gb_d = din("gb", [128, 2 * D])
    fnw_d = din("fnw", [128, D])
    wa_d = din("wa", [GK, D])
    wb_d = din("wb", [GK, D])
    wo_d = din("wo", [D, D])
    cst_d = din("cst", [128, cfg.NCST])
    out_d = T(nc.dram_tensor("out", [SEG, D], F32, kind="ExternalOutput").ap(), "out")
    of_d = [T(nc.dram_tensor("o_scr%d" % i, [SEG, 2 * GK], F32).ap(), "oscr") for i in range(2)]
    sk_d = T(nc.dram_tensor("scr_k", [2 * H, 128, NP * 128], BF16).ap(), "scr_k")
    sv_d = T(nc.dram_tensor("scr_v", [2 * H, 128, NP * 128], BF16).ap(), "scr_v")
    sq_d = T(nc.dram_tensor("scr_q", [2 * H, 128, SEG], BF16).ap(), "scr_q")
    PT = 3 * GK + 2 * D
    ptm_d = T(nc.dram_tensor("ptm", [SEG, PT], F32).ap(), "ptm")

    psf = [T(nc.alloc_psum_tensor("psf%d" % i, [128, 512], F32), "psf", x=True) for i in range(6)]
    psb = [T(nc.alloc_psum_tensor("psb%d" % i, [128, 1024], BF16), "psb", x=True) for i in range(2)]
    rr = {"f": 0, "b": 0, "fpool": list(range(6)), "bpool": [0, 1]}

    def PF():
        rr["f"] += 1
        return psf[rr["fpool"][rr["f"] % len(rr["fpool"])]]

    def PB():
        rr["b"] += 1
        return psb[rr["bpool"][rr["b"] % len(rr["bpool"])]]

    last_bp = {}

    def mm(o, o_ap, l, l_ap, r, r_ap, start=True, stop=True, bp=0):
        if last_bp.get(id(o), bp) != bp and o.w is not None and o.w[0].split("#")[0] == "pe":
            kk, vv = o.w
            if k.waited.get(("pe", kk), 0) < vv:
                nc.tensor.wait_ge(k.semh[kk], vv)
                k.waited[("pe", kk)] = vv
        last_bp[id(o)] = bp
        k.op("pe", lambda e: e.matmul(o_ap, lhsT=l_ap, rhs=r_ap, start=start, stop=stop), [l, r], [o])

    def tr(o, o_ap, i, i_ap, idt, id_ap):
        k.op("pe", lambda e: e.transpose(out=o_ap, in_=i_ap, identity=id_ap), [i, idt], [o])

    def act(o, o_ap, i, i_ap, func, bias=None, scale=None, extra=(), accum=None, eng="act", accum_t=None):
        kw = {}
        if bias is not None:
            kw["bias"] = bias
        if scale is not None:
            kw["scale"] = scale
        if accum is not None:
            kw["accum_out"] = accum
        k.op(eng, lambda e: e.activation(out=o_ap, in_=i_ap, func=func, **kw), [i] + list(extra),
             [o] + ([accum_t] if accum_t is not None else []))

    def tt(eng, o, o_ap, a, a_ap, b, b_ap, op):
        k.op(eng, lambda e: e.tensor_tensor(out=o_ap, in0=a_ap, in1=b_ap, op=op), [a, b], [o])

    def ts(eng, o, o_ap, a, a_ap, s1, op0, s2=None, op1=None, extra=()):
        if op1 is None:
            k.op(eng, lambda e: e.tensor_scalar(out=o_ap, in0=a_ap, scalar1=s1, scalar2=None, op0=op0),
                 [a] + list(extra), [o])
        else:
            k.op(eng, lambda e: e.tensor_scalar(out=o_ap, in0=a_ap, scalar1=s1, scalar2=s2, op0=op0, op1=op1),
                 [a] + list(extra), [o])

    def stt(eng, o, o_ap, a, a_ap, s, b, b_ap, op0, op1, extra=()):
        k.op(eng, lambda e: e.scalar_tensor_tensor(out=o_ap, in0=a_ap, scalar=s, in1=b_ap, op0=op0, op1=op1),
             [a, b] + list(extra), [o])

    def cp(eng, o, o_ap, i, i_ap):
        if eng == "act":
            k.op("act", lambda e: e.copy(out=o_ap, in_=i_ap), [i], [o])
        else:
            k.op(eng, lambda e: e.tensor_copy(out=o_ap, in_=i_ap), [i], [o])

    def mset(eng, o, o_ap, val):
        k.op(eng, lambda e: e.memset(o_ap, val), [], [o])

    cst = A.alloc([128, cfg.NCST], F32, "cst")
    k.dma("sp", "c_cst", cst, cst[:, :], cst_d, cst_d[:, :])
    identb = A.alloc([128, 128], BF16, "identb")
    onesb = A.alloc([128, 128], BF16, "onesb")
    cp("dve", identb, identb[:, :], cst, cst[:, cfg.c_ident:cfg.c_ident + 128])
    cp("dve", onesb, onesb[:, :], cst, cst[:, cfg.c_ones:cfg.c_ones + 128])
    Jb = A.alloc([128, 128], BF16, "Jb")
    cp("dve", Jb, Jb[:, :], cst, cst[:, cfg.c_J:cfg.c_J + 128])
    ident = lambda: cst[:, cfg.c_ident:cfg.c_ident + 128]
    trim = lambda: cst[:, cfg.c_tri:cfg.c_tri + 128]
    trims = lambda: cst[:, cfg.c_tris:cfg.c_tris + 128]
    mbs = lambda: cst[:, cfg.c_mbs:cfg.c_mbs + 128]
    mbi = lambda: cst[:, cfg.c_mbi:cfg.c_mbi + 128]
    selh = lambda h, b0=0: cst[b0:b0 + H, cfg.c_sel + h * 128: cfg.c_sel + (h + 1) * 128]
    idH = lambda b0: cst[b0:b0 + H, cfg.c_idH:cfg.c_idH + H]
    flags = A.alloc([128, 8], F32, "flags")
    k.dma("sp", "c_fl", flags, flags[:, :], flags_d, flags_d[:, :])

    SW = 132
    st_w = [A.alloc([128, SW], F32, "stw") for _ in range(2 * H)]
    st_f = [A.alloc([128, SW], F32, "stf") for _ in range(2 * H)]
    st_b = [A.alloc([128, SW], BF16, "stb") for _ in range(2 * H)]
    for t_ in st_w + st_f:
        mset("pool", t_, t_[:, :], 0.0)
    for t_ in st_b:
        mset("pool", t_, t_[:, :], 0.0)
    m_w = A.alloc([H, 4], F32, "m_w")
    mset("pool", m_w, m_w[:, :], 0.0)

    nTh = A.alloc([128, KC, 32], BF16, "nTh")

    def rms_to_nT(x_t, rows, dst_fn, nwr, scratch):
        junk, ss, xn = scratch
        act(junk, junk[0:rows, :], x_t, x_t[0:rows, :], AF.Square, accum=ss[0:rows, 0:1], accum_t=ss)
        act(ss, ss[0:rows, 1:2], ss, ss[0:rows, 0:1], AF.Ln, bias=float(NORM_EPS), scale=1.0 / D)
        act(ss, ss[0:rows, 2:3], ss, ss[0:rows, 1:2], AF.Exp, scale=-0.5)
        stt("dve", xn, xn[0:rows, :], x_t, x_t[0:rows, :], ss[0:rows, 2:3], nwr, nwr[0:rows, :],
            ALU.mult, ALU.mult, extra=[ss])
        for k0 in range(0, KC, 8):
            kn = min(8, KC - k0)
            pb = PB()
            for j in range(kn):
                tr(pb, pb[:, j * 128: j * 128 + rows], xn, xn[0:rows, (k0 + j) * 128:(k0 + j + 1) * 128],
                   identb, identb[0:rows, 0:rows])
            dst_fn(k0, kn, pb)

    mark0 = A.mark()

    nT = A.alloc([128, KC, SEG], BF16, "nT")
    nT_tiles = [T(nT.h, "nTp%d" % p) for p in range(NP)]
    R1 = A.alloc([72, SEG], F32, "R1")
    Mb = A.alloc([8, SEG], F32, "Mb")
    chs = A.alloc([8, 8, NCH], F32, "chs")
    NQT = 9
    tok = A.alloc([128, NP, NQT * H], F32, "tok")
    TQ_NGC, TQ_BG, TQ_KDS, TQ_EGC, TQ_BETA, TQ_DQ, TQ_KES, TQ_WINT, TQ_EMT = range(9)

    def tokc(p, q, h, r0=0, r1=128):
        return tok[r0:r1, p, q * H + h: q * H + h + 1]

    mark_slot = A.mark()

    for s in range(5):
        own = s >= 3
        mA = A.mark()
        xbuf = [A.alloc([128, D], F32, "xbuf") for _ in range(2)]
        scr = [(A.alloc([128, D], BF16, "junk"), A.alloc([128, 4], F32, "ssb"), A.alloc([128, D], BF16, "xn"))
               for _ in range(2)]
        junk, ssb, xn = scr[0]
        nwr = A.alloc([128, D], F32, "nwr")
        k.dma("sp", "c_nwr", nwr, nwr[:, :], nwr_d, nwr_d[:, :])
        if s == 0:
            xh_t = xbuf[1]
            k.dma("sp", "xl1", xh_t, xh_t[0:32, :], xh_d, xh_d[:, :])

            def dst_h(k0, kn, pb):
                cp("act", nTh, nTh[:, k0:k0 + kn, :],
                   pb, pb[:, 0:kn * 128].rearrange("p (k t) -> p k t", t=128)[:, :, 0:32])
            rms_to_nT(xh_t, 32, dst_h, nwr, (junk, ssb, xn))
        for p in range(NP):
            xb = xbuf[p % 2]
            k.dma("sp", "xl%d" % (p % 2), xb, xb[:, :], xs_d, xs_d[s, p * 128:(p + 1) * 128, :])

            def dst_p(k0, kn, pb, p=p):
                cp("act" if (k0 // 8) % 2 == 0 else "dve", nT_tiles[p], nT[:, k0:k0 + kn, p * 128:(p + 1) * 128],
                   pb, pb[:, 0:kn * 128].rearrange("p (k t) -> p k t", t=128))
            rms_to_nT(xb, 128, dst_p, nwr, scr[p % 2])
        k.barrier()
        k.mark("A%d" % s)
        A.release(mA)
        if os.environ.get("MK_STOP") == "A%d" % s:
            return nc

        mC = A.mark()
        HB = min(2, HG)
        FLIP = (s == 4) and not os.environ.get("MK_NOFLIP")
        wctr = [0]
        if not FLIP:
            wblk = [A.alloc([128, KC, HB * 128], BF16, "wblk") for _ in range(2)]
            pc = [A.alloc([128, SEG + 4], BF16, "pc")] * 2
            rinv = A.alloc([128, SEG], F32, "rinv")
            dg = A.alloc([128, 5, 128], BF16, "dg")
            post = [A.alloc([128, SEG], F32, "post")] * 2
            sqb = A.alloc([128, SEG], BF16, "sqb")
            vT = A.alloc([128, SEG], BF16, "vT")
        else:
            fkb = [A.alloc([128, NP, 128], BF16, "fk") for _ in range(2)]
            fvb = [A.alloc([128, NP, 128], BF16, "fv") for _ in range(2)]
            fqb = [A.alloc([128, SEG], BF16, "fq") for _ in range(2)]
            qtm = A.alloc([128, NP, 128], BF16, "qtm")
        HD = []

        def alloc_hd():
            hd = {"kT": [A.alloc([128, SEG], BF16, "kT") for _ in range(HG)],
                  "qT": [A.alloc([128, SEG], BF16, "qT") for _ in range(HG)] if own else None,
                  "ktm": [A.alloc([128, NP, 128], BF16, "ktm") for _ in range(HG)],
                  "vtm": [A.alloc([128, NP, 132], BF16, "vtm") for _ in range(HG)],
                  "chb": [A.alloc([128, 2, NCH], F32, "chb") for _ in range(HG)],
                  "cvw": A.alloc([128, 3 * HG, 5], F32, "cvw"), "idx": len(HD)}
            for hh in range(HG):
                mset("pool", hd["vtm"][hh], hd["vtm"][hh][:, :, 128:129], 1.0)
            HD.append(hd)
        alloc_hd()
        mX = A.mark()
        b_done = [False]
        wgb = A.alloc([128, KC, H4], BF16, "wgb")
        gpc = A.alloc([H4, 8], F32, "gpc")
        Yx = A.alloc([H4, SEG], F32, "Yx")
        Ye = A.alloc([H4, SEG], F32, "Ye")
        Yy = A.alloc([H4, SEG], F32, "Yy")
        PW = 96
        padA = A.alloc([H4, NCH, PW], F32, "padA")
        padB = A.alloc([H4, NCH, PW], F32, "padB")
        R3 = A.alloc([72, SEG], F32, "R3")
        R5 = A.alloc([8, SEG], F32, "R5")
        Mq = A.alloc([8, SEG], F32, "Mq")
        Me = A.alloc([8, SEG], F32, "Me")
        Mi = A.alloc([8, SEG], F32, "Mi")
        Mn = A.alloc([8, SEG], F32, "Mn")

        def gates_gen():
            k.dma("pool", "wg", wgb, wgb[:, :, :], wg_d, wg_d[s].rearrange("(kc p) c -> p kc c", p=128))
            k.dma("sp", "gp", gpc, gpc[:, :], gp_d, gp_d[s])
            act(gpc, gpc[:, 5:6], gpc, gpc[:, 1:2], AF.Exp)
            yield
            tt("dve", gpc, gpc[:, 5:6], gpc, gpc[:, 5:6], gpc, gpc[:, 3:4], ALU.mult)
            yield
            for tb in range(NTB):
                pf = PF()
                for kc in range(KC):
                    mm(pf, pf[0:H4, 0:TB], wgb, wgb[:, kc, :], nT, nT[:, kc, tb * TB:(tb + 1) * TB],
                       start=(kc == 0), stop=(kc == KC - 1))
                ts("dve", Yx, Yx[:, tb * TB:(tb + 1) * TB], pf, pf[0:H4, 0:TB], gpc[:, 0:1], ALU.add, extra=[gpc])
                yield
            act(Ye, Ye[:, :], Yx, Yx[:, :], AF.Exp, scale=gpc[:, 2:3], extra=[gpc])
            yield
            act(Ye, Ye[:, :], Ye, Ye[:, :], AF.Ln, bias=1.0)
            yield
            ts("dve", Yx, Yx[:, :], Yx, Yx[:, :], gpc[:, 4:5], ALU.mult, extra=[gpc])
            yield
            stt("dve", Yy, Yy[:, :], Ye, Ye[:, :], gpc[:, 5:6], Yx, Yx[:, :], ALU.mult, ALU.add, extra=[gpc])
            yield
            mset("pool", padA, padA[:, :, 0:32], 0.0)
            yield
            mset("pool", padB, padB[:, :, 0:32], 0.0)
            yield
            cp("dve", padA, padA[:, :, 32:96], Yy, Yy[:, :].rearrange("p (c t) -> p c t", t=64))
            yield
            a_, b_ = padA, padB
            for sh in (1, 2, 4, 8, 16, 32):
                tt("dve", b_, b_[:, :, 32:96], a_, a_[:, :, 32:96], a_, a_[:, :, 32 - sh:96 - sh], ALU.add)
                yield
                a_, b_ = b_, a_
            Cs = a_
            for tb in range(NTB):
                c0 = tb * (TB // 64)
                cs_ap = Cs[:, c0:c0 + TB // 64, 32:96]
                pf = PF()
                mm(pf, pf[0:72, 0:TB], cst, cst[0:H4, cfg.c_my1:cfg.c_my1 + 72], Yy, Yy[:, tb * TB:(tb + 1) * TB],
                   start=True, stop=False)
                mm(pf, pf[0:72, 0:TB], cst, cst[0:H4, cfg.c_mc1:cfg.c_mc1 + 72], Cs, cs_ap, start=False, stop=True)
                cp("act", R1, R1[:, tb * TB:(tb + 1) * TB], pf, pf[0:72, 0:TB])
                yield
                pf2 = PF()
                mm(pf2, pf2[0:H, 0:TB], cst, cst[0:H4, cfg.c_mc2:cfg.c_mc2 + H], Cs, cs_ap, start=True, stop=True)
                cp("act", Mb, Mb[0:H, tb * TB:(tb + 1) * TB], pf2, pf2[0:H, 0:TB])
                yield
                pf3 = PF()
                mm(pf3, pf3[0:H, 0:TB], cst, cst[0:H4, cfg.c_my2:cfg.c_my2 + H], Yy, Yy[:, tb * TB:(tb + 1) * TB],
                   start=True, stop=False)
                mm(pf3, pf3[0:H, 0:TB], cst, cst[0:H4, cfg.c_mc2 + 32:cfg.c_mc2 + 32 + H], Cs, cs_ap,
                   start=False, stop=True)
                cp("act", Mq, Mq[0:H, tb * TB:(tb + 1) * TB], pf3, pf3[0:H, 0:TB])
                yield
            c3 = lambda t_, r0=0: t_[r0:r0 + H, :].rearrange("p (c t) -> p c t", t=64)
            cp("dve", chs, chs[0:H, 0, :], R1, c3(R1)[:, :, 63])
            yield
            act(chs, chs[0:H, 1, :], chs, chs[0:H, 0, :], AF.Exp)
            yield
            act(R3, R3[0:H, :], R1, R1[0:H, :], AF.Exp)
            yield
            act(R3, R3[32:32 + H, :], R1, R1[32:32 + H, :], AF.Exp)
            yield
            act(R3, R3[64:64 + H, :], R1, R1[64:64 + H, :], AF.Exp)
            yield
            bc_ch = lambda q: chs[0:H, q:q + 1, :].rearrange("p o c -> p c o").to_broadcast([H, NCH, 64])
            tt("dve", R5, c3(R5), chs, bc_ch(0), R1, c3(R1), ALU.subtract)
            yield
            act(R5, R5[0:H, :], R5, R5[0:H, :], AF.Exp)
            yield
            cp("dve", chs, chs[0:H, 2, :], Mb, c3(Mb)[:, :, 63])
            yield
            tt("dve", Me, c3(Me), Mq, c3(Mq), chs, bc_ch(2), ALU.add)
            yield
            k.op("dve", lambda e: e.tensor_reduce(out=chs[0:H, 3, :], in_=c3(Me), axis=AX.X, op=ALU.max),
                 [Me], [chs])
            yield
            for c in range(NCH):
                cc = slice(c, c + 1)
                cp("dve", chs, chs[0:H, 4, cc], m_w, m_w[0:H, 0:1])
                yield
                tt("dve", m_w, m_w[0:H, 2:3], m_w, m_w[0:H, 0:1], chs, chs[0:H, 2, cc], ALU.add)
                yield
                tt("dve", m_w, m_w[0:H, 0:1], m_w, m_w[0:H, 2:3], chs, chs[0:H, 3, cc], ALU.max)
                yield
                tt("dve", chs, chs[0:H, 5, cc], m_w, m_w[0:H, 2:3], m_w, m_w[0:H, 0:1], ALU.subtract)
                yield
                cp("dve", chs, chs[0:H, 6, cc], m_w, m_w[0:H, 0:1])
                yield
            act(chs, chs[0:H, 5, :], chs, chs[0:H, 5, :], AF.Exp)
            yield
            tt("dve", Me, c3(Me), Me, c3(Me), chs, bc_ch(6), ALU.subtract)
            yield
            act(Me, Me[0:H, :], Me, Me[0:H, :], AF.Exp)
            yield
            if own:
                mset("pool", padA, padA[0:H, :, 0:32], NEG)
                yield
                mset("pool", padB, padB[0:H, :, 0:32], NEG)
                yield
                cp("dve", padA, padA[0:H, :, 32:96], Mq, c3(Mq))
                yield
                a_, b_ = padA, padB
                for sh in (1, 2, 4, 8, 16, 32):
                    tt("dve", b_, b_[0:H, :, 32:96], a_, a_[0:H, :, 32:96], a_, a_[0:H, :, 32 - sh:96 - sh], ALU.max)
                    yield
                    a_, b_ = b_, a_
                tt("dve", Mi, c3(Mi), Mb, c3(Mb), a_, a_[0:H, :, 32:96], ALU.add)
                yield
                tt("dve", Mn, c3(Mn), Mb, c3(Mb), chs, bc_ch(4), ALU.add)
                yield
                tt("dve", Mi, Mi[0:H, :], Mi, Mi[0:H, :], Mn, Mn[0:H, :], ALU.max)
                yield
                tt("dve", Mn, Mn[0:H, :], Mn, Mn[0:H, :], Mi, Mi[0:H, :], ALU.subtract)
                yield
                act(Mn, Mn[0:H, :], Mn, Mn[0:H, :], AF.Exp)
                yield
                tt("dve", Mb, Mb[0:H, :], Mb, Mb[0:H, :], Mi, Mi[0:H, :], ALU.subtract)
                yield
                act(Mi, Mi[0:H, :], Mi, Mi[0:H, :], AF.Exp, scale=-1.0)
                yield
            qsrc = [(TQ_NGC, R1, 0), (TQ_BG, R3, 32), (TQ_KDS, R5, 0), (TQ_EGC, R3, 0), (TQ_BETA, R3, 64),
                    (TQ_DQ, Mq, 0), (TQ_KES, Me, 0)]
            if own:
                qsrc += [(TQ_WINT, Mn, 0), (TQ_EMT, Mi, 0)]
            for p in range(NP):
                pf = PF()
                for (qi, rt, b0) in qsrc:
                    mm(pf, pf[:, qi * H:(qi + 1) * H], rt, rt[b0:b0 + H, p * 128:(p + 1) * 128], cst, idH(b0), bp=b0)
                nq = NQT if own else 7
                cp("act", tok, tok[:, p, 0:nq * H], pf, pf[:, 0:nq * H])
                yield
                ts("dve", tok, tok[:, p, 0:H], tok, tok[:, p, 0:H], -1.0, ALU.mult)
                yield

            b_done[0] = True
            yield

        def prep_gen(br, g, hd):
            kT, qT, ktm, vtm, chb, cvw = hd["kT"], hd["qT"], hd["ktm"], hd["vtm"], hd["chb"], hd["cvw"]
            if br == 0:
                kinds = [("k", cfg.o_gqkv + GK + g * WB, H + g * HG), ("v", cfg.o_gqkv + 2 * GK + g * WB, 2 * H + g * HG)]
                if own:
                    kinds.append(("q", cfg.o_gqkv + g * WB, g * HG))
                cv_d = cvg_d
            else:
                kinds = [("k", cfg.o_mqk + GK + g * WB, H + g * HG), ("v", cfg.o_mv + g * WB, None)]
                if own:
                    kinds.append(("q", cfg.o_mqk + g * WB, g * HG))
                cv_d = cvm_d
            if FLIP:
                kinds = []
                for hh in range(HG):
                    hs_ = br * H + g * HG + hh
                    fk, fv, fq = fkb[hh % 2], fvb[hh % 2], fqb[hh % 2]
                    k.dma("sp", "flk%d" % (hh % 2), fk, fk[:, :, :], sk_d, sk_d[hs_].rearrange("p (n c) -> p n c", c=128))
                    k.dma("sp", "flv%d" % (hh % 2), fv, fv[:, :, :], sv_d, sv_d[hs_].rearrange("p (n c) -> p n c", c=128))
                    k.dma("sp", "flq%d" % (hh % 2), fq, fq[:, :], sq_d, sq_d[hs_])
                    for p0 in range(0, NP, 4):
                        pn = min(4, NP - p0)
                        pf = PF()
                        for j in range(pn):
                            mm(pf, pf[:, j * 128:(j + 1) * 128], Jb, Jb[:, :], fk, fk[:, NP - 1 - (p0 + j), :])
                        cp("act", ktm[hh], ktm[hh][:, p0:p0 + pn, :], pf, pf[:, 0:pn * 128].rearrange("p (n c) -> p n c", c=128))
                        pf = PF()
                        for j in range(pn):
                            mm(pf, pf[:, j * 128:(j + 1) * 128], fk, fk[:, NP - 1 - (p0 + j), :], Jb, Jb[:, :])
                        cp("dve", kT[hh], kT[hh][:, p0 * 128:(p0 + pn) * 128], pf, pf[:, 0:pn * 128])
                        yield
                        pf = PF()
                        for j in range(pn):
                            mm(pf, pf[:, j * 128:(j + 1) * 128], Jb, Jb[:, :], fv, fv[:, NP - 1 - (p0 + j), :])
                        cp("act", vtm[hh], vtm[hh][:, p0:p0 + pn, 0:128], pf, pf[:, 0:pn * 128].rearrange("p (n c) -> p n c", c=128))
                        yield
                    for p0 in range(0, NP, 8):
                        pn = min(8, NP - p0)
                        pb = PB()
                        for j in range(pn):
                            tr(pb, pb[:, j * 128:(j + 1) * 128], fq, fq[:, (p0 + j) * 128:(p0 + j + 1) * 128], identb, identb[:, :])
                        cp("act", qtm, qtm[:, p0:p0 + pn, :], pb, pb[:, 0:pn * 128].rearrange("p (k t) -> p k t", t=128))
                        yield
                    for p0 in range(0, NP, 4):
                        pn = min(4, NP - p0)
                        pf = PF()
                        for j in range(pn):
                            mm(pf, pf[:, j * 128:(j + 1) * 128], qtm, qtm[:, NP - 1 - (p0 + j), :], Jb, Jb[:, :])
                        cp("dve", qT[hh], qT[hh][:, p0 * 128:(p0 + pn) * 128], pf, pf[:, 0:pn * 128])
                        yield
            for ki, (kind, col0, cg0) in enumerate(kinds):
                if cg0 is not None:
                    k.dma("sp", "cv%d" % hd["idx"], cvw, cvw[:, ki * HG:(ki + 1) * HG, :], cv_d, cv_d[s, :, cg0:cg0 + HG, :])
            for ki, (kind, col0, cg0) in enumerate(kinds):
                for hh in range(HG):
                    if hh % HB == 0:
                        wctr[0] += 1
                        wbk = wblk[wctr[0] % 2]
                        k.dma("pool", "wb%d" % (wctr[0] % 2), wbk, wbk[:, :, :], win_d,
                              win_d[:, col0 + hh * 128:col0 + (hh + HB) * 128].rearrange("(kc p) c -> p kc c", p=128))
                    hw_ = hh % HB
                    pcb = pc[hh % 2]
                    pob = post[hh % 2]
                    noconv = (br == 1 and kind == "v")
                    for tb in range(NTB):
                        pf = PF()
                        for kc in range(KC):
                            mm(pf, pf[:, 0:TB], wbk, wbk[:, kc, hw_ * 128:(hw_ + 1) * 128],
                               nT, nT[:, kc, tb * TB:(tb + 1) * TB], start=(kc == 0), stop=(kc == KC - 1))
                        if noconv:
                            cp("act", vT, vT[:, tb * TB:(tb + 1) * TB], pf, pf[:, 0:TB])
                        else:
                            cp("act", pcb, pcb[:, 2 + tb * TB: 2 + (tb + 1) * TB], pf, pf[:, 0:TB])
                        yield
                    if noconv:
                        src_bf = vT
                    else:
                        pf = PF()
                        for kc in range(KC):
                            mm(pf, pf[:, 0:32], wbk, wbk[:, kc, hw_ * 128:(hw_ + 1) * 128], nTh, nTh[:, kc, :],
                               start=(kc == 0), stop=(kc == KC - 1))
                        cp("act", pcb, pcb[:, 0:2], pf, pf[:, 4 * s:4 * s + 2])
                        cp("act", pcb, pcb[:, SEG + 2:SEG + 4], pf, pf[:, 4 * s + 2:4 * s + 4])
                        cw = lambda t_: cvw[:, ki * HG + hh, t_:t_ + 1]
                        for t_ in range(5):
                            ts("dve", dg, dg[:, t_, :], identb, identb[:, :], cw(t_), ALU.mult, extra=[cvw])
                        for tb in range(NTB):
                            pf = PF()
                            for t_ in range(5):
                                mm(pf, pf[:, 0:TB], dg, dg[:, t_, :], pcb, pcb[:, t_ + tb * TB: t_ + (tb + 1) * TB],
                                   start=(t_ == 0), stop=(t_ == 4))
                            act(pob, pob[:, tb * TB:(tb + 1) * TB], pf, pf[:, 0:TB], AF.Silu)
                            yield
                        dst = {"k": kT[hh], "v": vT, "q": qT[hh] if own else None}[kind]
                        if br == 0 and kind in ("k", "q"):
                            act(sqb, sqb[:, :], pob, pob[:, :], AF.Square)
                            for tb in range(NTB):
                                pf = PF()
                                sl = slice(tb * TB, (tb + 1) * TB)
                                mm(pf, pf[:, 0:TB], onesb, onesb[:, :], sqb, sqb[:, sl])
                                act(rinv, rinv[:, sl], pf, pf[:, 0:TB], AF.Ln, bias=float(NORM_EPS))
                                act(rinv, rinv[:, sl], rinv, rinv[:, sl], AF.Exp, scale=-0.5)
                                yield
                            sc = 1.0 if kind == "k" else 128.0 ** -0.5
                            stt("dve", dst, dst[:, :], pob, pob[:, :], sc, rinv, rinv[:, :], ALU.mult, ALU.mult)
                        elif br == 1 and kind == "k":
                            ts("dve", dst, dst[:, :], pob, pob[:, :], 128.0 ** -0.5, ALU.mult)
                        else:
                            cp("dve", dst, dst[:, :], pob, pob[:, :])
                        src_bf = dst
                    if kind in ("k", "v"):
                        dtm = ktm[hh] if kind == "k" else vtm[hh]
                        for p0 in range(0, NP, 8):
                            pn = min(8, NP - p0)
                            pb = PB()
                            for j in range(pn):
                                tr(pb, pb[:, j * 128:(j + 1) * 128], src_bf,
                                   src_bf[:, (p0 + j) * 128:(p0 + j + 1) * 128], identb, identb[:, :])
                            cp("act", dtm, dtm[:, p0:p0 + pn, 0:128],
                               pb, pb[:, 0:pn * 128].rearrange("p (k t) -> p k t", t=128))
                            yield
            if s == 3 and not os.environ.get("MK_NOFLIP"):
                for hh in range(HG):
                    hs_ = br * H + g * HG + hh
                    k.dma("sp", "stk%d" % hs_, sk_d, sk_d[hs_].rearrange("p (n c) -> p n c", c=128), ktm[hh], ktm[hh][:, :, :])
                    k.dma("sp", "stv%d" % hs_, sv_d, sv_d[hs_].rearrange("p (n c) -> p n c", c=128), vtm[hh], vtm[hh][:, :, 0:128])
                    k.dma("sp", "stq%d" % hs_, sq_d, sq_d[hs_], qT[hh], qT[hh][:, :])
            while not b_done[0]:
                yield
            for hh in range(HG):
                h = g * HG + hh
                pf = PF()
                if br == 0:
                    mm(pf, pf[:, 0:NCH], cst, selh(h), chs, chs[0:H, 1, :])
                    cp("act", chb[hh], chb[hh][:, 0, :], pf, pf[:, 0:NCH])
                else:
                    mm(pf, pf[:, 0:NCH], cst, selh(h), chs, chs[0:H, 5, :])
                    cp("act", chb[hh], chb[hh][:, 0, :], pf, pf[:, 0:NCH])

            yield

        def scan_gens(br, g, hd):
            kT, qT, ktm, vtm, chb = hd["kT"], hd["qT"], hd["ktm"], hd["vtm"], hd["chb"]

            prod_done = [0] * HG
            cons_done = [0] * HG

            def prod_gdn(hh):
                h = g * HG + hh
                d = wk[hh]
                B = psf[hh]
                for p in range(NP):
                    b_ = p % 2
                    while p >= cons_done[hh] + 2:
                        yield
                    psl = slice(p * 128, (p + 1) * 128)
                    mm(B, B[:, 0:128], kT[hh], kT[hh][:, psl], kT[hh], kT[hh][:, psl])
                    if own:
                        mm(B, B[:, 128:256], kT[hh], kT[hh][:, psl], qT[hh], qT[hh][:, psl])
                    mm(B, B[:, 256:384], cst, selh(h, 32), R1, R1[32:32 + H, psl], bp=32)
                    if own:
                        mm(B, B[:, 384:512], cst, selh(h), R1, R1[0:H, psl])
                    ts("pool", d["kd"][b_], d["kd"][b_][:, 0:128], ktm[hh], ktm[hh][:, p, :], tokc(p, TQ_KDS, h), ALU.mult, extra=[tok])
                    ts("pool", d["vbb"], d["vbb"][:, 0:128], vtm[hh], vtm[hh][:, p, 0:128], tokc(p, TQ_BETA, h), ALU.mult, extra=[tok])
                    ts("pool", d["kbg"], d["kbg"][:, 0:128], ktm[hh], ktm[hh][:, p, :], tokc(p, TQ_BG, h), ALU.mult, extra=[tok])
                    yield
                    stt("dve", d["eu"], d["eu"][:, :], B, B[:, 256:384], tokc(p, TQ_NGC, h), cst, mbs(), ALU.add, ALU.add, extra=[tok])
                    yield
                    act(d["eu"], d["eu"][:, :], d["eu"], d["eu"][:, :], AF.Exp)
                    yield
                    tt("dve", d["UL"], d["UL"][:, 0, 0:128], d["eu"], d["eu"][:, :], B, B[:, 0:128], ALU.mult)
                    yield
                    if own:
                        stt("dve", d["gm"], d["gm"][:, :], B, B[:, 384:512], tokc(p, TQ_NGC, h), cst, mbi(), ALU.add, ALU.add, extra=[tok])
                        yield
                        act(d["gm"], d["gm"][:, :], d["gm"], d["gm"][:, :], AF.Exp)
                        yield
                        tt("dve", d["AT"][b_], d["AT"][b_][:, 0:128], d["gm"], d["gm"][:, :], B, B[:, 128:256], ALU.mult)
                    ULc, ULn = d["UL"], d["UL2"]
                    pb = psb[0]
                    po_ = hh * 128
                    tr(pb, pb[:, po_:po_ + 128], ULc, ULc[:, 0, 0:128], identb, identb[:, :])
                    tt("dve", d["P"], d["P"][:, 0:128], identb, identb[:, :], ULc, ULc[:, 0, 0:128], ALU.subtract)
                    yield
                    cp("act", ULc, ULc[:, 1, 0:128], pb, pb[:, po_:po_ + 128])
                    yield
                    Pc, Pn = d["P"], d["P2"]
                    U_ = lambda t_: t_[:, 0, 0:128]
                    L_ = lambda t_: t_[:, 1, 0:128]
                    mm(B, B[:, 0:128], ULc, L_(ULc), ULc, U_(ULc))
                    mm(B, B[:, 128:256], ULc, U_(ULc), ULc, L_(ULc))
                    yield
                    cp("act", ULn, ULn[:, :, 0:128], B, B[:, 0:256].rearrange("p (n c) -> p n c", c=128))
                    yield
                    ULc, ULn = ULn, ULc
                    for lvl in range(4):
                        last = (lvl == 3)
                        if not last:
                            mm(B, B[:, 0:128], ULc, L_(ULc), ULc, U_(ULc))
                        mm(B, B[:, 128:256], ULc, U_(ULc), ULc, L_(ULc))
                        mm(B, B[:, 256:384], ULc, L_(ULc), Pc, Pc[:, 0:128])
                        yield
                        if not last:
                            cp("act", ULn, ULn[:, :, 0:128], B, B[:, 0:256].rearrange("p (n c) -> p n c", c=128))
                        else:
                            cp("act", ULn, ULn[:, 1, 0:128], B, B[:, 128:256])
                        tt("dve", Pn, Pn[:, 0:128], B, B[:, 256:384], Pc, Pc[:, 0:128], ALU.add)
                        yield
                        ULc, ULn = ULn, ULc
                        Pc, Pn = Pn, Pc
                    Lc = ULc
                    mm(B, B[:, 0:128], Lc, L_(Lc), Pc, Pc[:, 0:128])
                    yield
                    tt("dve", Pn, Pn[:, 0:128], B, B[:, 0:128], Pc, Pc[:, 0:128], ALU.add)
                    TT = Pn
                    yield
                    mm(B, B[:, 0:128], TT, TT[:, 0:128], d["vbb"], d["vbb"][:, 0:128])
                    mm(B, B[:, 128:256], d["kbg"], d["kbg"][:, 0:128], TT, TT[:, 0:128])
                    yield
                    cp("act", d["ub"][b_], d["ub"][b_][:, :], B, B[:, 0:128])
                    cp("dve", d["wT"][b_], d["wT"][b_][:, 0:128], B, B[:, 128:256])
                    prod_done[hh] = p + 1
                    yield

            def cons_gdn(hh):
                h = g * HG + hh
                hs = br * H + h
                d = wk[hh]
                Sw, Sb = st_w[hs], st_b[hs]
                C = psf[4 + hh // 2]
                c0 = (hh % 2) * 256
                for p in range(NP):
                    b_ = p % 2
                    psl = slice(p * 128, (p + 1) * 128)
                    while prod_done[hh] <= p:
                        yield
                    for ci in range(2):
                        c = 2 * p + ci
                        rs = slice(ci * 64, ci * 64 + 64)
                        mm(C, C[:, c0:c0 + 128], d["wT"][b_], d["wT"][b_][:, 0:128], Sb, Sb[:, 0:128])
                        if own:
                            mm(C, C[:, c0 + 128:c0 + 256], qT[hh], qT[hh][:, psl], Sb, Sb[:, 0:128])
                        yield
                        tt("dve", d["vnew"], d["vnew"][rs, 0:128], d["ub"][b_], d["ub"][b_][rs, :], C, C[rs, c0:c0 + 128], ALU.subtract)
                        if own:
                            act(d["t1"], d["t1"][rs, 0:128], C, C[rs, c0 + 128:c0 + 256], AF.Copy,
                                scale=tokc(p, TQ_EGC, h, rs.start, rs.stop), extra=[tok])
                        yield
                        mm(C, C[:, c0:c0 + 128], d["kd"][b_], d["kd"][b_][rs, 0:128], d["vnew"], d["vnew"][rs, 0:128], bp=rs.start)
                        if own:
                            mm(C, C[:, c0 + 128:c0 + 256], d["AT"][b_], d["AT"][b_][rs, 0:128], d["vnew"], d["vnew"][rs, 0:128], bp=rs.start)
                        yield
                        stt("dve", Sw, Sw[:, 0:128], Sw, Sw[:, 0:128], chb[hh][:, 0, c:c + 1], C, C[:, c0:c0 + 128],
                            ALU.mult, ALU.add, extra=[chb[hh]])
                        if own:
                            tt("dve", d["o"], d["o"][rs, :], C, C[rs, c0 + 128:c0 + 256], d["t1"], d["t1"][rs, 0:128], ALU.add)
                        yield
                        cp("act", Sb, Sb[:, 0:128], Sw, Sw[:, 0:128])
                        yield
                    if own:
                        od = of_d[s - 3]
                        k.dma("sp", "ost%d" % hh, od, od[psl, hs * 128:(hs + 1) * 128], d["o"], d["o"][:, :])
                    cons_done[hh] = p + 1

            def scan_mlstm(hh):
                h = g * HG + hh
                hs = br * H + h
                d = wk[hh]
                Sw, Sb = st_w[hs], st_b[hs]
                for p in range(NP):
                    psl = slice(p * 128, (p + 1) * 128)
                    if own:
                        pg = psf[hh]
                        mm(pg, pg[:, 0:128], kT[hh], kT[hh][:, psl], qT[hh], qT[hh][:, psl])
                        mm(pg, pg[:, 128:256], cst, selh(h), Mb, Mb[0:H, psl])
                    act(d["kd"][0], d["kd"][0][:, 0:128], ktm[hh], ktm[hh][:, p, :], AF.Copy, scale=tokc(p, TQ_KES, h), extra=[tok])
                    yield
                    if own:
                        stt("dve", d["eu"], d["eu"][:, :], pg, pg[:, 128:256], tokc(p, TQ_DQ, h), cst, mbi(), ALU.add, ALU.add, extra=[tok])
                        yield
                        act(d["eu"], d["eu"][:, :], d["eu"], d["eu"][:, :], AF.Exp)
                        yield
                        tt("dve", d["AT"][0], d["AT"][0][:, 0:128], d["eu"], d["eu"][:, :], pg, pg[:, 0:128], ALU.mult)
                        yield
                    for ci in range(2):
                        c = 2 * p + ci
                        rs = slice(ci * 64, ci * 64 + 64)
                        if own:
                            po = psf[hh]
                            mm(po, po[:, 0:129], qT[hh], qT[hh][:, psl], Sb, Sb[:, 0:129])
                            mm(po, po[:, 130:259], d["AT"][0], d["AT"][0][rs, 0:128], vtm[hh], vtm[hh][rs, p, 0:129], bp=rs.start)
                        pu = psf[hh]
                        mm(pu, pu[:, 260:389], d["kd"][0], d["kd"][0][rs, 0:128], vtm[hh], vtm[hh][rs, p, 0:129], bp=rs.start)
                        yield
                        if own:
                            act(d["t1"], d["t1"][rs, 0:129], po, po[rs, 0:129], AF.Copy,
                                scale=tokc(p, TQ_WINT, h, rs.start, rs.stop), extra=[tok])
                            yield
                            tt("dve", d["t1"], d["t1"][rs, 0:129], po, po[rs, 130:259], d["t1"], d["t1"][rs, 0:129], ALU.add)
                            yield
                        stt("dve", Sw, Sw[:, 0:129], Sw, Sw[:, 0:129], chb[hh][:, 0, c:c + 1], pu, pu[:, 260:389],
                            ALU.mult, ALU.add, extra=[chb[hh]])
                        yield
                        cp("act", Sb, Sb[:, 0:129], Sw, Sw[:, 0:129])
                        yield
                    if own:
                        ts("dve", d["dm"], d["dm"][:, 0:1], d["t1"], d["t1"][:, 128:129], -1.0, ALU.mult,
                           d["t1"][:, 128:129], ALU.max)
                        yield
                        ts("dve", d["dm"], d["dm"][:, 0:1], d["dm"], d["dm"][:, 0:1], tokc(p, TQ_EMT, h), ALU.max, extra=[tok])
                        yield
                        k.op("dve", lambda e: e.reciprocal(out=d["dm"][:, 1:2], in_=d["dm"][:, 0:1]), [d["dm"]], [d["dm"]])
                        yield
                        ts("dve", d["o"], d["o"][:, :], d["t1"], d["t1"][:, 0:128], d["dm"][:, 1:2], ALU.mult, extra=[d["dm"]])
                        yield
                        od = of_d[s - 3]
                        k.dma("sp", "ost%d" % hh, od, od[psl, hs * 128:(hs + 1) * 128], d["o"], d["o"][:, :])


            if br == 0:
                return [prod_gdn(hh) for hh in range(HG)] + [cons_gdn(hh) for hh in range(HG)]
            return [scan_mlstm(hh) for hh in range(HG)]

        groups = [(br_, g_) for br_ in range(2) for g_ in range(NG)]
        STAG = int(os.environ.get("MK_STAG", "3"))
        CDEL = int(os.environ.get("MK_CDEL", "24"))
        BPER = int(os.environ.get("MK_BPER", "4"))
        bgn = gates_gen()
        pgn = prep_gen(groups[0][0], groups[0][1], HD[0])
        while bgn is not None or pgn is not None:
            if bgn is not None:
                try:
                    for _ in range(BPER):
                        next(bgn)
                except StopIteration:
                    bgn = None
            if pgn is not None:
                try:
                    next(pgn)
                except StopIteration:
                    pgn = None
        k.barrier()
        k.mark("B%d" % s)
        A.release(mX)
        alloc_hd()
        wk = []
        for hh in range(HG):
            d = {}
            d["eu"] = A.alloc([128, 128], F32, "eu")
            d["gm"] = A.alloc([128, 128], F32, "gm")
            for nm in ("P", "P2", "vnew", "vbb", "kbg"):
                d[nm] = A.alloc([128, 132], BF16, nm)
            d["UL"] = A.alloc([128, 2, 132], BF16, "UL")
            d["UL2"] = A.alloc([128, 2, 132], BF16, "UL2")
            for nm in ("AT", "kd", "wT"):
                d[nm] = [A.alloc([128, 132], BF16, nm) for _ in range(2)]
            d["ub"] = [A.alloc([128, 128], F32, "ub") for _ in range(2)]
            d["t1"] = A.alloc([128, 132], F32, "t1")
            d["t2"] = A.alloc([128, 132], F32, "t2")
            d["o"] = A.alloc([128, 128], F32, "o")
            d["dm"] = A.alloc([128, 4], F32, "dm")
            wk.append(d)
        for gi, (br_, g_) in enumerate(groups):
            if gi > 0:
                for _ in prep_gen(br_, g_, HD[gi % 2]):
                    pass
            gens = scan_gens(br_, g_, HD[gi % 2])
            offs = [(i % HG) * STAG + (CDEL if i >= HG else 0) for i in range(len(gens))]
            live = list(range(len(gens)))
            tstep = 0
            while live:
                for i in list(live):
                    if tstep < offs[i]:
                        continue
                    try:
                        next(gens[i])
                    except StopIteration:
                        live.remove(i)
                tstep += 1
            k.mark("C%d%d%d" % (s, br_, g_))
        rr["fpool"], rr["bpool"] = list(range(6)), [0, 1]
        k.barrier()
        A.release(mC)
        if os.environ.get("MK_STOP") == "C%d" % s:
            return nc

        fl = lambda j: flags[:, j:j + 1]
        flh = lambda j: flags[0:H, j:j + 1]
        if s in (0, 1):
            for hs in range(2 * H):
                stt("dve", st_f[hs], st_f[hs][:, :], st_w[hs], st_w[hs][:, :], fl(s), st_f[hs], st_f[hs][:, :],
                    ALU.mult, ALU.add, extra=[flags])
                ts("dve", st_w[hs], st_w[hs][:, :], st_w[hs], st_w[hs][:, :], fl(2 + s), ALU.mult, extra=[flags])
                cp("act", st_b[hs], st_b[hs][:, :], st_w[hs], st_w[hs][:, :])
            stt("dve", m_w, m_w[0:H, 1:2], m_w, m_w[0:H, 0:1], flh(s), m_w, m_w[0:H, 1:2], ALU.mult, ALU.add, extra=[flags])
            ts("dve", m_w, m_w[0:H, 0:1], m_w, m_w[0:H, 0:1], flh(2 + s), ALU.mult, extra=[flags])
        elif s == 2:
            for hs in range(2 * H):
                stt("dve", st_f[hs], st_f[hs][:, :], st_w[hs], st_w[hs][:, :], fl(4), st_f[hs], st_f[hs][:, :],
                    ALU.mult, ALU.add, extra=[flags])
                ts("dve", st_w[hs], st_w[hs][:, :], st_w[hs], st_w[hs][:, :], fl(5), ALU.mult, extra=[flags])
            tmp = A.alloc([128, SW], F32, "swp")
            for hs in range(2 * H):
                cp("dve", tmp, tmp[:, :], st_w[hs], st_w[hs][:, :])
                cp("dve", st_w[hs], st_w[hs][:, :], st_f[hs], st_f[hs][:, :])
                cp("dve", st_f[hs], st_f[hs][:, :], tmp, tmp[:, :])
                cp("act", st_b[hs], st_b[hs][:, :], st_w[hs], st_w[hs][:, :])
            stt("dve", m_w, m_w[0:H, 1:2], m_w, m_w[0:H, 0:1], flh(4), m_w, m_w[0:H, 1:2], ALU.mult, ALU.add, extra=[flags])
            ts("dve", m_w, m_w[0:H, 0:1], m_w, m_w[0:H, 0:1], flh(5), ALU.mult, extra=[flags])
            cp("dve", m_w, m_w[0:H, 2:3], m_w, m_w[0:H, 0:1])
            cp("dve", m_w, m_w[0:H, 0:1], m_w, m_w[0:H, 1:2])
            cp("dve", m_w, m_w[0:H, 1:2], m_w, m_w[0:H, 2:3])
        elif s == 3:
            for hs in range(2 * H):
                cp("dve", st_w[hs], st_w[hs][:, :], st_f[hs], st_f[hs][:, :])
                cp("act", st_b[hs], st_b[hs][:, :], st_w[hs], st_w[hs][:, :])
            cp("dve", m_w, m_w[0:H, 0:1], m_w, m_w[0:H, 1:2])
        k.barrier()

    if os.environ.get("MK_STOP") == "S":
        return nc
    A.release(mark0)
    k.barrier()
    yT = A.alloc([128, 2 * H, SEG], BF16, "yT")
    m_top = A.mark()
    nT = A.alloc([128, KC, SEG], BF16, "nTo")
    mE = A.mark()
    xbo = [A.alloc([128, D], F32, "xbo") for _ in range(2)]
    scr = [(A.alloc([128, D], BF16, "junk"), A.alloc([128, 4], F32, "ssb"), A.alloc([128, D], BF16, "xn"))
           for _ in range(2)]
    nwr = A.alloc([128, D], F32, "nwr")
    k.dma("sp", "c_nwr", nwr, nwr[:, :], nwr_d, nwr_d[:, :])
    for p in range(NP):
        xb = xbo[p % 2]
        k.dma("sp", "xl%d" % (p % 2), xb, xb[:, :], xs_d, xs_d[3, p * 128:(p + 1) * 128, :])

        def dst_o(k0, kn, pb, p=p):
            cp("act" if (k0 // 8) % 2 == 0 else "dve", nT, nT[:, k0:k0 + kn, p * 128:(p + 1) * 128],
               pb, pb[:, 0:kn * 128].rearrange("p (k t) -> p k t", t=128))
        rms_to_nT(xb, 128, dst_o, nwr, scr[p % 2])
    k.barrier()
    A.release(mE)

    k.mark("E0")
    CB = min(512, GK)
    wtm = [A.alloc([128, KC, CB], BF16, "wtm") for _ in range(2)]
    gbr = A.alloc([128, 2 * D], F32, "gbr")
    k.dma("sp", "c_gbr", gbr, gbr[:, :], gb_d, gb_d[:, :])
    evb = [A.alloc([128, CB], F32, "evb") for _ in range(3)]
    blocks = []
    for c0 in range(0, GK, CB):
        blocks.append((cfg.o_gz + c0, 0 * GK + c0, "silu", None))
    for c0 in range(0, GK, CB):
        blocks.append((cfg.o_mo + c0, 1 * GK + c0, "sig", None))
    for c0 in range(0, GK, CB):
        blocks.append((cfg.o_mz + c0, 2 * GK + c0, "silu", None))
    n_a = len(blocks)
    for c0 in range(0, 2 * D, CB):
        blocks.append((cfg.o_gates + c0, 3 * GK + c0, "sig", c0))
    eic = [0]

    def e1_gen(lo, hi):
        for bi in range(lo, hi):
            wc0, pc0, fn, gb0 = blocks[bi]
            wt = wtm[bi % 2]
            k.dma("pool", "wt%d" % (bi % 2), wt, wt[:, :, :], win_d,
                  win_d[:, wc0:wc0 + CB].rearrange("(kc p) c -> p kc c", p=128))
            for p in range(NP):
                pf = PF()
                for kc in range(KC):
                    mm(pf, pf[:, 0:CB], nT, nT[:, kc, p * 128:(p + 1) * 128], wt, wt[:, kc, :],
                       start=(kc == 0), stop=(kc == KC - 1))
                ev = evb[eic[0] % 3]
                eic[0] += 1
                if gb0 is not None:
                    tt("dve", ev, ev[:, :], pf, pf[:, 0:CB], gbr, gbr[:, gb0:gb0 + CB], ALU.add)
                    act(ev, ev[:, :], ev, ev[:, :], AF.Sigmoid)
                else:
                    act(ev, ev[:, :], pf, pf[:, 0:CB], AF.Silu if fn == "silu" else AF.Sigmoid)
                k.dma("sp", "pst%d" % ((eic[0] - 1) % 3), ptm_d, ptm_d[p * 128:(p + 1) * 128, pc0:pc0 + CB], ev, ev[:, :])
                yield

    for _ in e1_gen(0, n_a):
        pass
    k.mark("E1")
    gnw = A.alloc([128, GK], F32, "gnw")
    mnw = A.alloc([128, GK], F32, "mnw")
    k.dma("sp", "c_gnw", gnw, gnw[:, :], gnw_d, gnw_d[:, :])
    k.dma("sp", "c_mnw", mnw, mnw[:, :], mnw_d, mnw_d[:, :])
    Jf = lambda: cst[:, cfg.c_J:cfg.c_J + 128]
    ofb = A.alloc([128, 2 * GK], F32, "ofb")
    obb = A.alloc([128, 2 * GK], F32, "obb")
    zb = A.alloc([128, 3 * GK], F32, "zb")
    st4 = A.alloc([128, 4, 2 * H], F32, "st4")
    yb = A.alloc([128, 2 * GK], BF16, "yb")
    ptm_z = T(ptm_d.h, "ptm_z")
    ptm_z.w = ptm_d.w

    def e2_gen():
        for p in range(NP):
            of_, ob_, z_ = ofb, obb, zb
            k.dma("pool", "ld_of", of_, of_[:, :], of_d[0], of_d[0][p * 128:(p + 1) * 128, :])
            k.dma("pool", "ld_ob", ob_, ob_[:, :], of_d[1], of_d[1][(NP - 1 - p) * 128:(NP - p) * 128, :])
            k.dma("pool", "ld_z", z_, z_[:, :], ptm_z, ptm_d[p * 128:(p + 1) * 128, 0:3 * GK])
            yield
            for c0 in range(0, 2 * GK, 512):
                cn = min(512, 2 * GK - c0)
                pf = PF()
                mm(pf, pf[:, 0:cn], cst, Jf(), ob_, ob_[:, c0:c0 + cn])
                tt("dve", of_, of_[:, c0:c0 + cn], pf, pf[:, 0:cn], of_, of_[:, c0:c0 + cn], ALU.add)
                yield
            o3 = of_[:, :].rearrange("p (h d) -> p h d", d=128)
            sq = ob_
            act(sq, sq[:, :], of_, of_[:, :], AF.Square)
            yield
            k.op("dve", lambda e: e.tensor_reduce(out=st4[:, 0, :], in_=sq[:, :].rearrange("p (h d) -> p h d", d=128),
                                                 axis=AX.X, op=ALU.add), [sq], [st4])
            yield
            k.op("dve", lambda e: e.tensor_reduce(out=st4[:, 1, :], in_=o3, axis=AX.X, op=ALU.add), [of_], [st4])
            ts("dve", st4, st4[:, 1, 0:H], st4, st4[:, 1, 0:H], 0.0, ALU.mult)
            ts("dve", st4, st4[:, 1, :], st4, st4[:, 1, :], 1.0 / 128.0, ALU.mult)
            tt("dve", st4, st4[:, 2, :], st4, st4[:, 1, :], st4, st4[:, 1, :], ALU.mult)
            stt("dve", st4, st4[:, 2, :], st4, st4[:, 0, :], 1.0 / 128.0, st4, st4[:, 2, :], ALU.mult, ALU.subtract)
            ts("dve", st4, st4[:, 2, :], st4, st4[:, 2, :], 0.0, ALU.max)
            act(st4, st4[:, 2, :], st4, st4[:, 2, :], AF.Ln, bias=float(NORM_EPS))
            act(st4, st4[:, 2, :], st4, st4[:, 2, :], AF.Exp, scale=-0.5)
            yield
            tt("dve", of_, o3, of_, o3, st4, st4[:, 1, :].rearrange("p (h o) -> p h o", o=1).to_broadcast([128, 2 * H, 128]),
               ALU.subtract)
            yield
            tt("dve", of_, o3, of_, o3, st4, st4[:, 2, :].rearrange("p (h o) -> p h o", o=1).to_broadcast([128, 2 * H, 128]),
               ALU.mult)
            yield
            tt("dve", of_, of_[:, 0:GK], of_, of_[:, 0:GK], gnw, gnw[:, :], ALU.mult)
            yield
            tt("dve", of_, of_[:, GK:2 * GK], of_, of_[:, GK:2 * GK], mnw, mnw[:, :], ALU.mult)
            yield
            tt("pool", z_, z_[:, GK:2 * GK], z_, z_[:, GK:2 * GK], z_, z_[:, 2 * GK:3 * GK], ALU.mult)
            yield
            tt("dve", yb, yb[:, :], of_, of_[:, :], z_, z_[:, 0:2 * GK], ALU.mult)
            yield
            for k0 in range(0, 2 * H, 8):
                kn = min(8, 2 * H - k0)
                pb = PB()
                for j in range(kn):
                    tr(pb, pb[:, j * 128:(j + 1) * 128], yb, yb[:, (k0 + j) * 128:(k0 + j + 1) * 128], identb, identb[:, :])
                cp("act", yT, yT[:, k0:k0 + kn, p * 128:(p + 1) * 128],
                   pb, pb[:, 0:kn * 128].rearrange("p (k t) -> p k t", t=128))
                yield

    g1 = e1_gen(n_a, len(blocks))
    g2 = e2_gen()
    E2PER = int(os.environ.get("MK_E2PER", "3"))
    while g1 is not None or g2 is not None:
        if g1 is not None:
            try:
                next(g1)
            except StopIteration:
                g1 = None
        if g2 is not None:
            try:
                for _ in range(E2PER):
                    next(g2)
            except StopIteration:
                g2 = None
    k.barrier()
    A.release(m_top)

    k.mark("E2")
    mT = A.alloc([128, KC, SEG], BF16, "mT")
    mE = A.mark()
    DB = min(512, D)
    wab = [A.alloc([128, 2 * H, DB], BF16, "wab")] * 2
    g01 = [A.alloc([128, 2, DB], F32, "g01") for _ in range(2)]
    mg = A.alloc([128, DB], F32, "mg")
    mgb = A.alloc([128, DB], BF16, "mgb")
    gi = 0
    for bi, d0 in enumerate(range(0, D, DB)):
        wt = wab[bi % 2]
        k.dma("pool", "wab%d" % (bi % 2), wt, wt[:, 0:H, :], wa_d,
              wa_d[:, d0:d0 + DB].rearrange("(kc p) c -> p kc c", p=128))
        k.dma("pool", "wab%d" % (bi % 2), wt, wt[:, H:2 * H, :], wb_d,
              wb_d[:, d0:d0 + DB].rearrange("(kc p) c -> p kc c", p=128))
        for p in range(NP):
            gt = g01[gi % 2]
            gi += 1
            k.dma("sp", "ld_g%d" % (gi % 2), gt, gt[:, 0, :], ptm_d, ptm_d[p * 128:(p + 1) * 128, 3 * GK + d0:3 * GK + d0 + DB])
            k.dma("sp", "ld_g%d" % (gi % 2), gt, gt[:, 1, :], ptm_d,
                  ptm_d[p * 128:(p + 1) * 128, 3 * GK + D + d0:3 * GK + D + d0 + DB])
            pfa = PF()
            for kc in range(H):
                mm(pfa, pfa[:, 0:DB], yT, yT[:, kc, p * 128:(p + 1) * 128], wt, wt[:, kc, :], start=(kc == 0), stop=(kc == H - 1))
            pfb = PF()
            for kc in range(H):
                mm(pfb, pfb[:, 0:DB], yT, yT[:, H + kc, p * 128:(p + 1) * 128], wt, wt[:, H + kc, :],
                   start=(kc == 0), stop=(kc == H - 1))
            tt("dve", mg, mg[:, :], pfa, pfa[:, 0:DB], gt, gt[:, 0, :], ALU.mult)
            tt("dve", gt, gt[:, 1, :], pfb, pfb[:, 0:DB], gt, gt[:, 1, :], ALU.mult)
            tt("pool", mgb, mgb[:, :], mg, mg[:, :], gt, gt[:, 1, :], ALU.add)
            pb = PB()
            nk = DB // 128
            for j in range(nk):
                tr(pb, pb[:, j * 128:(j + 1) * 128], mgb, mgb[:, j * 128:(j + 1) * 128], identb, identb[:, :])
            cp("act", mT, mT[:, d0 // 128:d0 // 128 + nk, p * 128:(p + 1) * 128],
               pb, pb[:, 0:nk * 128].rearrange("p (k t) -> p k t", t=128))
    k.barrier()
    A.release(mE)

    k.mark("E3")
    mE = A.mark()
    xo = A.alloc([128, NP, D], F32, "xo")
    for p in range(NP):
        k.dma("sp", "xl%d" % (p % 2), xo, xo[:, p, :], xs_d, xs_d[3, p * 128:(p + 1) * 128, :])
    wob = [A.alloc([128, KC, DB], BF16, "wob") for _ in range(2)]
    for bi, d0 in enumerate(range(0, D, DB)):
        wt = wob[bi % 2]
        k.dma("pool", "wob%d" % (bi % 2), wt, wt[:, :, :], wo_d,
              wo_d[:, d0:d0 + DB].rearrange("(kc p) c -> p kc c", p=128))
        for p in range(NP):
            pf = PF()
            for kc in range(KC):
                mm(pf, pf[:, 0:DB], mT, mT[:, kc, p * 128:(p + 1) * 128], wt, wt[:, kc, :], start=(kc == 0), stop=(kc == KC - 1))
            tt("dve", xo, xo[:, p, d0:d0 + DB], pf, pf[:, 0:DB], xo, xo[:, p, d0:d0 + DB], ALU.add)
    k.barrier()
    fnw = A.alloc([128, D], F32, "fnw")
    k.dma("sp", "c_fnw", fnw, fnw[:, :], fnw_d, fnw_d[:, :])
    junkf = A.alloc([128, D], BF16, "junkf")
    ss2 = A.alloc([128, 4], F32, "ss2")
    for p in range(NP):
        act(junkf, junkf[:, :], xo, xo[:, p, :], AF.Square, accum=ss2[:, 0:1], accum_t=ss2)
        act(ss2, ss2[:, 1:2], ss2, ss2[:, 0:1], AF.Ln, bias=float(NORM_EPS), scale=1.0 / D)
        act(ss2, ss2[:, 2:3], ss2, ss2[:, 1:2], AF.Exp, scale=-0.5)
        stt("dve", xo, xo[:, p, :], xo, xo[:, p, :], ss2[:, 2:3], fnw, fnw[:, :], ALU.mult, ALU.mult, extra=[ss2])
    for p in range(NP):
        k.dma("sp", "fin", out_d, out_d[p * 128:(p + 1) * 128, :], xo, xo[:, p, :])
    k.wait_all("sp", [out_d])
    k.barrier()
    k.mark("END")
    build_nc.last_k = k
    return nc


def make_consts(cfg):
    H = cfg.H
    c = np.zeros((128, cfg.NCST), np.float32)
    idx = np.arange(128)
    c[:, cfg.c_ident:cfg.c_ident + 128] = np.eye(128, dtype=np.float32)
    c[:, cfg.c_J:cfg.c_J + 128] = np.eye(128, dtype=np.float32)[::-1]
    same = (idx[:, None] // 64) == (idx[None, :] // 64)
    c[:, cfg.c_tri:cfg.c_tri + 128] = (same & (idx[None, :] >= idx[:, None])).astype(np.float32)
    c[:, cfg.c_tris:cfg.c_tris + 128] = (same & (idx[None, :] > idx[:, None])).astype(np.float32)
    c[:, cfg.c_ones:cfg.c_ones + 128] = 1.0
    c[:, cfg.c_mbs:cfg.c_mbs + 128] = np.where(same & (idx[None, :] > idx[:, None]), 0.0, -30000.0)
    c[:, cfg.c_mbi:cfg.c_mbi + 128] = np.where(same & (idx[None, :] >= idx[:, None]), 0.0, -30000.0)
    for h in range(H):
        c[h, cfg.c_sel + h * 128: cfg.c_sel + (h + 1) * 128] = 1.0
        c[32 + h, cfg.c_sel + h * 128: cfg.c_sel + (h + 1) * 128] = 1.0
        c[64 + h, cfg.c_sel + h * 128: cfg.c_sel + (h + 1) * 128] = 1.0
        c[h, cfg.c_idH + h] = 1.0
        c[32 + h, cfg.c_idH + h] = 1.0
        c[64 + h, cfg.c_idH + h] = 1.0
    for h in range(H):
        c[H + h, cfg.c_mc1 + h] = 1.0
        c[H + h, cfg.c_mc1 + 32 + h] = 1.0
        c[h, cfg.c_my1 + 32 + h] = 1.0
        c[h, cfg.c_my1 + 64 + h] = 1.0
        c[3 * H + h, cfg.c_mc2 + h] = 1.0
        c[2 * H + h, cfg.c_my2 + h] = 1.0
        c[3 * H + h, cfg.c_mc2 + 32 + h] = -1.0
    return c


def prep_core(inp, cfg, b, tq):
    D, H, SEG, GK = cfg.D, cfg.H, cfg.SEG, cfg.GK
    SEQ = 4 * SEG
    x = inp["x"][b]
    w_in = np.ascontiguousarray(inp["w_in"][0])
    slots = []
    for s in range(3):
        slots.append((s, 0) if s < tq else (3 + tq - s, 1))
    slots += [(tq, 0), (tq, 1)]
    xs = np.zeros((5, SEG, D), np.float32)
    xh = np.zeros((32, D), np.float32)
    ext = np.zeros((SEQ + 4, D), np.float32)
    ext[2:SEQ + 2] = x
    wg = np.zeros((5, D, 4 * H), np.float32)
    gp = np.zeros((5, 4 * H, 8), np.float32)
    cvg = np.zeros((5, 128, 3 * H, 5), np.float32)
    cvm = np.zeros((5, 128, 2 * H, 5), np.float32)
    cg = inp["conv_gdn"][0]
    cm = inp["conv_mlstm"][0]
    for s, (seg, dr) in enumerate(slots):
        e = ext[seg * SEG: seg * SEG + SEG + 4]
        if dr == 1:
            e = e[::-1]
        xs[s] = e[2:SEG + 2]
        xh[4 * s: 4 * s + 2] = e[0:2]
        xh[4 * s + 2: 4 * s + 4] = e[SEG + 2: SEG + 4]
        sl = slice(dr * H, (dr + 1) * H)
        wg[s, :, 0:H] = w_in[:, cfg.o_gbeta:cfg.o_gbeta + 2 * H][:, sl]
        wg[s, :, H:2 * H] = w_in[:, cfg.o_ga:cfg.o_ga + 2 * H][:, sl]
        wg[s, :, 2 * H:3 * H] = w_in[:, cfg.o_mi:cfg.o_mi + 2 * H][:, sl]
        wg[s, :, 3 * H:4 * H] = w_in[:, cfg.o_mf:cfg.o_mf + 2 * H][:, sl]
        gp[s, H:2 * H, 0] = inp["gdn_dt_bias"][0, dr]
        gp[s, 2 * H:3 * H, 0] = inp["mlstm_i_bias"][0, dr]
        gp[s, 3 * H:4 * H, 0] = inp["mlstm_f_bias"][0, dr]
        gp[s, H:2 * H, 1] = inp["gdn_a_log"][0, dr]
        gp[s, 0:H, 2] = -1.0
        gp[s, H:2 * H, 2] = 1.0
        gp[s, 3 * H:4 * H, 2] = -1.0
        gp[s, 0:H, 3] = -1.0
        gp[s, H:2 * H, 3] = -1.0
        gp[s, 3 * H:4 * H, 3] = -1.0
        gp[s, 2 * H:3 * H, 4] = 1.0
        cgs = cg[::-1] if dr == 1 else cg
        cms = cm[::-1] if dr == 1 else cm
        cvg[s] = cgs.T.reshape(3 * H, 128, 5).transpose(1, 0, 2)
        cvm[s] = cms.T.reshape(2 * H, 128, 5).transpose(1, 0, 2)
    flags = np.zeros((128, 8), np.float32)
    sw0, sw1, f3 = float(tq == 1), float(tq == 2), float(tq == 3)
    flags[:, 0], flags[:, 1], flags[:, 2], flags[:, 3] = sw0, sw1, 1.0 - sw0, 1.0 - sw1
    flags[:, 4], flags[:, 5] = f3, 1.0 - f3
    rep = lambda v: np.ascontiguousarray(np.broadcast_to(np.asarray(v, np.float32)[None, :], (128, len(v))))
    return {
        "xs": xs, "xh": xh, "w_in": w_in, "wg": wg, "gp": gp, "cvg": np.ascontiguousarray(cvg),
        "cvm": np.ascontiguousarray(cvm), "flags": flags,
        "nwr": rep(inp["norm_w"][0]),
        "gnw": rep(np.tile(inp["gdn_norm_w"][0], H)),
        "mnw": rep(inp["mlstm_norm_w"][0]),
        "gb": rep(inp["gate_bias"][0]),
        "fnw": rep(inp["final_norm_w"]),
        "wa": np.ascontiguousarray(inp["w_branch_gdn"][0]),
        "wb": np.ascontiguousarray(inp["w_branch_mlstm"][0]),
        "wo": np.ascontiguousarray(inp["w_out"][0]),
        "cst": make_consts(cfg),
    }


def kernel(**inputs):
    cfg = Cfg()
    inp = {k_: np.asarray(v, dtype=np.float32) for k_, v in inputs.items()}
    nc = build_nc(cfg)
    in_maps = [prep_core(inp, cfg, c // 4, c % 4) for c in range(N_CORES)]
    res = run_bass_kernel_spmd(nc, in_maps, core_ids=list(range(N_CORES)))
    out = np.zeros((2, 4 * cfg.SEG, cfg.D), np.float32)
    for c in range(N_CORES):
        b, tq = c // 4, c % 4
        out[b, tq * cfg.SEG:(tq + 1) * cfg.SEG] = res.results[c]["out"]
    return out
```

```python
import os
import numpy as np
import concourse.bass as bass
import concourse.mybir as mybir
from concourse.bass_utils import run_bass_kernel_spmd

F32 = mybir.dt.float32
BF16 = mybir.dt.bfloat16
AF = mybir.ActivationFunctionType
ALU = mybir.AluOpType
AX = mybir.AxisListType

N_CORES = 8
NORM_EPS = 1e-6
NEG = -1.0e30


class Cfg:
    def __init__(self, D=2048, H=8, SEG=1024):
        self.D, self.H, self.SEG = D, H, SEG
        self.KC = D // 128
        self.GK = H * 128
        self.NP = SEG // 128
        self.NCH = SEG // 64
        self.TB = min(512, SEG)
        self.NTB = SEG // self.TB
        self.HG = min(4, H)
        self.NG = H // self.HG
        self.WB = self.HG * 128
        GK = self.GK
        sizes = [3 * GK, GK, 2 * H, 2 * H, 2 * GK, GK, GK, GK, 2 * H, 2 * H, 2 * D]
        offs = np.concatenate([[0], np.cumsum(sizes)]).astype(int)
        (self.o_gqkv, self.o_gz, self.o_gbeta, self.o_ga, self.o_mqk, self.o_mv, self.o_mo,
         self.o_mz, self.o_mi, self.o_mf, self.o_gates) = [int(v) for v in offs[:-1]]
        self.IN_COLS = int(offs[-1])
        c = 0
        self.c_ident = c; c += 128
        self.c_J = c; c += 128
        self.c_tri = c; c += 128
        self.c_tris = c; c += 128
        self.c_ones = c; c += 128
        self.c_sel = c; c += H * 128
        self.c_idH = c; c += H
        self.c_mbs = c; c += 128
        self.c_mbi = c; c += 128
        self.c_my1 = c; c += 72
        self.c_mc1 = c; c += 72
        self.c_my2 = c; c += 72
        self.c_mc2 = c; c += 72
        self.NCST = c


class T:
    __slots__ = ("h", "w", "r", "name", "x")

    def __init__(self, h, name="", x=False):
        self.h = h
        self.w = None
        self.r = {}
        self.name = name
        self.x = x

    def __getitem__(self, idx):
        return self.h[idx]


class K:
    SAME_ENGINE_SYNC = ("act", "dve", "pool")
    EPOCH = 16000

    def __init__(self, nc):
        self.nc = nc
        self.eng = {"pe": nc.tensor, "act": nc.scalar, "dve": nc.vector,
                    "pool": nc.gpsimd, "sp": nc.sync}
        self.semh = {}
        self.cnt = {}
        self.epoch = {}
        self.waited = {}
        self.nops = {e: 0 for e in self.eng}
        self.marks = []

    def _cur(self, base):
        ep = self.epoch.get(base, 0)
        key = "%s#%d" % (base, ep)
        if key not in self.semh:
            self.semh[key] = self.nc.alloc_semaphore(name="s_" + key.replace("#", "_"))
            self.cnt[key] = 0
        elif self.cnt[key] >= self.EPOCH:
            self.epoch[base] = ep + 1
            return self._cur(base)
        return key

    @staticmethod
    def _need(needs, rec):
        if rec is None:
            return
        k, v = rec
        if needs.get(k, 0) < v:
            needs[k] = v

    def _waits(self, e, reads, writes):
        needs = {}
        for t in reads:
            self._need(needs, t.w)
        for t in writes:
            self._need(needs, t.w)
            for k, v in t.r.items():
                self._need(needs, (k, v))
        for k, v in needs.items():
            if k.split("#")[0] == e and e not in self.SAME_ENGINE_SYNC:
                continue
            if self.waited.get((e, k), 0) >= v:
                continue
            if k.startswith("d_") and v != self.cnt[k]:
                raise RuntimeError("ambiguous DMA wait on %s: %d of %d issued" % (k, v, self.cnt[k]))
            self.eng[e].wait_ge(self.semh[k], v)
            self.waited[(e, k)] = v

    def op(self, e, fn, reads=(), writes=()):
        self.total = getattr(self, "total", 0) + 1
        if self.total > int(os.environ.get("MK_MAXOPS", "1000000000")):
            return None
        xr = [t for t in reads if t.x]
        if xr:
            reads = [t for t in reads if not t.x]
            writes = list(writes) + xr
        if os.environ.get("MK_GLOCK") and any(t.x and (os.environ["MK_GLOCK"] in ("1", t.name)) for t in writes):
            if not hasattr(self, "glock"):
                self.glock = T(None, "glock")
            writes = list(writes) + [self.glock]
        self._waits(e, reads, writes)
        key = self._cur(e)
        ins = fn(self.eng[e])
        self.cnt[key] += 1
        self.nops[e] += 1
        ins.then_inc(self.semh[key], 1)
        c = self.cnt[key]
        for t in reads:
            if t.r.get(key, 0) < c:
                t.r[key] = c
        for t in writes:
            t.w = (key, c)
            t.r = {}
        return ins

    def dma(self, q, name, out_t, out_ap, in_t, in_ap, **kw):
        if getattr(self, "total", 0) > int(os.environ.get("MK_MAXOPS", "1000000000")):
            return None
        self._waits(q, [in_t], [out_t])
        key = self._cur("d_" + name)
        ins = self.eng[q].dma_start(out=out_ap, in_=in_ap, **kw)
        ins.then_inc(self.semh[key], 16)
        self.cnt[key] += 16
        c = self.cnt[key]
        in_t.r[key] = c
        out_t.w = (key, c)
        out_t.r = {}
        return ins

    def wait_all(self, e, tiles):
        self._waits(e, tiles, [])

    def mark(self, label):
        self.marks.append((label, dict(self.nops)))

    def barrier(self):
        allkeys = [(k, v) for k, v in self.cnt.items() if v > 0]
        for e in ("pe", "act", "dve", "pool", "sp"):
            for k, v in allkeys:
                if k.split("#")[0] == e and e not in self.SAME_ENGINE_SYNC:
                    continue
                if self.waited.get((e, k), 0) >= v:
                    continue
                self.eng[e].wait_ge(self.semh[k], v)
                self.waited[(e, k)] = v


class Arena:
    def __init__(self, nc):
        self.nc = nc
        total = nc.SBUF_PARTITION_SIZE_BYTES
        self.base = ((total - nc.sbuf_bytes_remaining + 63) // 64) * 64
        self.limit = total - 256
        self.ptr = self.base
        self.n = 0

    def alloc(self, shape, dtype, name="t"):
        nb = 4 if dtype == F32 else 2
        sz = nb
        for s in shape[1:]:
            sz *= s
        sz = ((sz + 63) // 64) * 64
        if self.ptr + sz > self.limit:
            raise RuntimeError("SBUF arena overflow at %s: %d + %d > %d" % (name, self.ptr, sz, self.limit))
        self.n += 1
        h = self.nc.alloc_sbuf_tensor_at("%s_%d" % (name, self.n), list(shape), dtype, offset=self.ptr)
        self.ptr += sz
        return T(h, name)

    def mark(self):
        return self.ptr

    def release(self, mark):
        self.ptr = mark


def build_nc(cfg):
    D, H, SEG, KC, GK, NP, NCH = cfg.D, cfg.H, cfg.SEG, cfg.KC, cfg.GK, cfg.NP, cfg.NCH
    TB, NTB, HG, NG, WB = cfg.TB, cfg.NTB, cfg.HG, cfg.NG, cfg.WB
    H4 = 4 * H
    nc = bass.Bass("TRN2", target_bir_lowering=False)
    k = K(nc)
    A = Arena(nc)

    def din(name, shape):
        return T(nc.dram_tensor(name, list(shape), F32, kind="ExternalInput").ap(), name)

    xs_d = din("xs", [5, SEG, D])
    xh_d = din("xh", [32, D])
    win_d = din("w_in", [D, cfg.IN_COLS])
    wg_d = din("wg", [5, D, H4])
    gp_d = din("gp", [5, H4, 8])
    cvg_d = din("cvg", [5, 128, 3 * H, 5])
    cvm_d = din("cvm", [5, 128, 2 * H, 5])
    flags_d = din("flags", [128, 8])
    nwr_d = din("nwr", [128, D])
    gnw_d = din("gnw", [128, GK])
    mnw_d = din("mnw", [128, GK])
    gb_d = din("gb", [128, 2 * D])
    fnw_d = din("fnw", [128, D])
    wa_d = din("wa", [GK, D])
    wb_d = din("wb", [GK, D])
    wo_d = din("wo", [D, D])
    cst_d = din("cst", [128, cfg.NCST])
    out_d = T(nc.dram_tensor("out", [SEG, D], F32, kind="ExternalOutput").ap(), "out")
    of_d = [T(nc.dram_tensor("o_scr%d" % i, [SEG, 2 * GK], F32).ap(), "oscr") for i in range(2)]
    sk_d = T(nc.dram_tensor("scr_k", [2 * H, 128, NP * 128], BF16).ap(), "scr_k")
    sv_d = T(nc.dram_tensor("scr_v", [2 * H, 128, NP * 128], BF16).ap(), "scr_v")
    sq_d = T(nc.dram_tensor("scr_q", [2 * H, 128, SEG], BF16).ap(), "scr_q")
    PT = 3 * GK + 2 * D
    ptm_d = T(nc.dram_tensor("ptm", [SEG, PT], F32).ap(), "ptm")

    psf = [T(nc.alloc_psum_tensor("psf%d" % i, [128, 512], F32), "psf", x=True) for i in range(6)]
    psb = [T(nc.alloc_psum_tensor("psb%d" % i, [128, 1024], BF16), "psb", x=True) for i in range(2)]
    rr = {"f": 0, "b": 0, "fpool": list(range(6)), "bpool": [0, 1]}

    def PF():
        rr["f"] += 1
        return psf[rr["fpool"][rr["f"] % len(rr["fpool"])]]

    def PB():
        rr["b"] += 1
        return psb[rr["bpool"][rr["b"] % len(rr["bpool"])]]

    last_bp = {}

    def mm(o, o_ap, l, l_ap, r, r_ap, start=True, stop=True, bp=0):
        if last_bp.get(id(o), bp) != bp and o.w is not None and o.w[0].split("#")[0] == "pe":
            kk, vv = o.w
            if k.waited.get(("pe", kk), 0) < vv:
                nc.tensor.wait_ge(k.semh[kk], vv)
                k.waited[("pe", kk)] = vv
        last_bp[id(o)] = bp
        k.op("pe", lambda e: e.matmul(o_ap, lhsT=l_ap, rhs=r_ap, start=start, stop=stop), [l, r], [o])

    def tr(o, o_ap, i, i_ap, idt, id_ap):
        k.op("pe", lambda e: e.transpose(out=o_ap, in_=i_ap, identity=id_ap), [i, idt], [o])

    def act(o, o_ap, i, i_ap, func, bias=None, scale=None, extra=(), accum=None, eng="act", accum_t=None):
        kw = {}
        if bias is not None:
            kw["bias"] = bias
        if scale is not None:
            kw["scale"] = scale
        if accum is not None:
            kw["accum_out"] = accum
        k.op(eng, lambda e: e.activation(out=o_ap, in_=i_ap, func=func, **kw), [i] + list(extra),
             [o] + ([accum_t] if accum_t is not None else []))

    def tt(eng, o, o_ap, a, a_ap, b, b_ap, op):
        k.op(eng, lambda e: e.tensor_tensor(out=o_ap, in0=a_ap, in1=b_ap, op=op), [a, b], [o])

    def ts(eng, o, o_ap, a, a_ap, s1, op0, s2=None, op1=None, extra=()):
        if op1 is None:
            k.op(eng, lambda e: e.tensor_scalar(out=o_ap, in0=a_ap, scalar1=s1, scalar2=None, op0=op0),
                 [a] + list(extra), [o])
        else:
            k.op(eng, lambda e: e.tensor_scalar(out=o_ap, in0=a_ap, scalar1=s1, scalar2=s2, op0=op0, op1=op1),
                 [a] + list(extra), [o])

    def stt(eng, o, o_ap, a, a_ap, s, b, b_ap, op0, op1, extra=()):
        k.op(eng, lambda e: e.scalar_tensor_tensor(out=o_ap, in0=a_ap, scalar=s, in1=b_ap, op0=op0, op1=op1),
             [a, b] + list(extra), [o])

    def cp(eng, o, o_ap, i, i_ap):
        if eng == "act":
            k.op("act", lambda e: e.copy(out=o_ap, in_=i_ap), [i], [o])
        else:
            k.op(eng, lambda e: e.tensor_copy(out=o_ap, in_=i_ap), [i], [o])

    def mset(eng, o, o_ap, val):
        k.op(eng, lambda e: e.memset(o_ap, val), [], [o])

    cst = A.alloc([128, cfg.NCST], F32, "cst")
    k.dma("sp", "c_cst", cst, cst[:, :], cst_d, cst_d[:, :])
    identb = A.alloc([128, 128], BF16, "identb")
    onesb = A.alloc([128, 128], BF16, "onesb")
    cp("dve", identb, identb[:, :], cst, cst[:, cfg.c_ident:cfg.c_ident + 128])
    cp("dve", onesb, onesb[:, :], cst, cst[:, cfg.c_ones:cfg.c_ones + 128])
    Jb = A.alloc([128, 128], BF16, "Jb")
    cp("dve", Jb, Jb[:, :], cst, cst[:, cfg.c_J:cfg.c_J + 128])
    ident = lambda: cst[:, cfg.c_ident:cfg.c_ident + 128]
    trim = lambda: cst[:, cfg.c_tri:cfg.c_tri + 128]
    trims = lambda: cst[:, cfg.c_tris:cfg.c_tris + 128]
    mbs = lambda: cst[:, cfg.c_mbs:cfg.c_mbs + 128]
    mbi = lambda: cst[:, cfg.c_mbi:cfg.c_mbi + 128]
    selh = lambda h, b0=0: cst[b0:b0 + H, cfg.c_sel + h * 128: cfg.c_sel + (h + 1) * 128]
    idH = lambda b0: cst[b0:b0 + H, cfg.c_idH:cfg.c_idH + H]
    flags = A.alloc([128, 8], F32, "flags")
    k.dma("sp", "c_fl", flags, flags[:, :], flags_d, flags_d[:, :])

    SW = 132
    st_w = [A.alloc([128, SW], F32, "stw") for _ in range(2 * H)]
    st_f = [A.alloc([128, SW], F32, "stf") for _ in range(2 * H)]
    st_b = [A.alloc([128, SW], BF16, "stb") for _ in range(2 * H)]
    for t_ in st_w + st_f:
        mset("pool", t_, t_[:, :], 0.0)
    for t_ in st_b:
        mset("pool", t_, t_[:, :], 0.0)
    m_w = A.alloc([H, 4], F32, "m_w")
    mset("pool", m_w, m_w[:, :], 0.0)

    nTh = A.alloc([128, KC, 32], BF16, "nTh")

    def rms_to_nT(x_t, rows, dst_fn, nwr, scratch):
        junk, ss, xn = scratch
        act(junk, junk[0:rows, :], x_t, x_t[0:rows, :], AF.Square, accum=ss[0:rows, 0:1], accum_t=ss)
        act(ss, ss[0:rows, 1:2], ss, ss[0:rows, 0:1], AF.Ln, bias=float(NORM_EPS), scale=1.0 / D)
        act(ss, ss[0:rows, 2:3], ss, ss[0:rows, 1:2], AF.Exp, scale=-0.5)
        stt("dve", xn, xn[0:rows, :], x_t, x_t[0:rows, :], ss[0:rows, 2:3], nwr, nwr[0:rows, :],
            ALU.mult, ALU.mult, extra=[ss])
        for k0 in range(0, KC, 8):
            kn = min(8, KC - k0)
            pb = PB()
            for j in range(kn):
                tr(pb, pb[:, j * 128: j * 128 + rows], xn, xn[0:rows, (k0 + j) * 128:(k0 + j + 1) * 128],
                   identb, identb[0:rows, 0:rows])
            dst_fn(k0, kn, pb)

    mark0 = A.mark()

    nT = A.alloc([128, KC, SEG], BF16, "nT")
    nT_tiles = [T(nT.h, "nTp%d" % p) for p in range(NP)]
    R1 = A.alloc([72, SEG], F32, "R1")
    Mb = A.alloc([8, SEG], F32, "Mb")
    chs = A.alloc([8, 8, NCH], F32, "chs")
    NQT = 9
    tok = A.alloc([128, NP, NQT * H], F32, "tok")
    TQ_NGC, TQ_BG, TQ_KDS, TQ_EGC, TQ_BETA, TQ_DQ, TQ_KES, TQ_WINT, TQ_EMT = range(9)

    def tokc(p, q, h, r0=0, r1=128):
        return tok[r0:r1, p, q * H + h: q * H + h + 1]

    mark_slot = A.mark()

    for s in range(5):
        own = s >= 3
        mA = A.mark()
        xbuf = [A.alloc([128, D], F32, "xbuf") for _ in range(2)]
        scr = [(A.alloc([128, D], BF16, "junk"), A.alloc([128, 4], F32, "ssb"), A.alloc([128, D], BF16, "xn"))
               for _ in range(2)]
        junk, ssb, xn = scr[0]
        nwr = A.alloc([128, D], F32, "nwr")
        k.dma("sp", "c_nwr", nwr, nwr[:, :], nwr_d, nwr_d[:, :])
        if s == 0:
            xh_t = xbuf[1]
            k.dma("sp", "xl1", xh_t, xh_t[0:32, :], xh_d, xh_d[:, :])

            def dst_h(k0, kn, pb):
                cp("act", nTh, nTh[:, k0:k0 + kn, :],
                   pb, pb[:, 0:kn * 128].rearrange("p (k t) -> p k t", t=128)[:, :, 0:32])
            rms_to_nT(xh_t, 32, dst_h, nwr, (junk, ssb, xn))
        for p in range(NP):
            xb = xbuf[p % 2]
            k.dma("sp", "xl%d" % (p % 2), xb, xb[:, :], xs_d, xs_d[s, p * 128:(p + 1) * 128, :])

            def dst_p(k0, kn, pb, p=p):
                cp("act" if (k0 // 8) % 2 == 0 else "dve", nT_tiles[p], nT[:, k0:k0 + kn, p * 128:(p + 1) * 128],
                   pb, pb[:, 0:kn * 128].rearrange("p (k t) -> p k t", t=128))
            rms_to_nT(xb, 128, dst_p, nwr, scr[p % 2])
        k.barrier()
        k.mark("A%d" % s)
        A.release(mA)
        if os.environ.get("MK_STOP") == "A%d" % s:
            return nc

        mC = A.mark()
        HB = min(2, HG)
        FLIP = (s == 4) and not os.environ.get("MK_NOFLIP")
        wctr = [0]
        if not FLIP:
            wblk = [A.alloc([128, KC, HB * 128], BF16, "wblk") for _ in range(2)]
            pc = [A.alloc([128, SEG + 4], BF16, "pc")] * 2
            rinv = A.alloc([128, SEG], F32, "rinv")
            dg = A.alloc([128, 5, 128], BF16, "dg")
            post = [A.alloc([128, SEG], F32, "post")] * 2
            sqb = A.alloc([128, SEG], BF16, "sqb")
            vT = A.alloc([128, SEG], BF16, "vT")
        else:
            fkb = [A.alloc([128, NP, 128], BF16, "fk") for _ in range(2)]
            fvb = [A.alloc([128, NP, 128], BF16, "fv") for _ in range(2)]
            fqb = [A.alloc([128, SEG], BF16, "fq") for _ in range(2)]
            qtm = A.alloc([128, NP, 128], BF16, "qtm")
        HD = []

        def alloc_hd():
            hd = {"kT": [A.alloc([128, SEG], BF16, "kT") for _ in range(HG)],
                  "qT": [A.alloc([128, SEG], BF16, "qT") for _ in range(HG)] if own else None,
                  "ktm": [A.alloc([128, NP, 128], BF16, "ktm") for _ in range(HG)],
                  "vtm": [A.alloc([128, NP, 132], BF16, "vtm") for _ in range(HG)],
                  "chb": [A.alloc([128, 2, NCH], F32, "chb") for _ in range(HG)],
                  "cvw": A.alloc([128, 3 * HG, 5], F32, "cvw"), "idx": len(HD)}
            for hh in range(HG):
                mset("pool", hd["vtm"][hh], hd["vtm"][hh][:, :, 128:129], 1.0)
            HD.append(hd)
        alloc_hd()
        mX = A.mark()
        b_done = [False]
        wgb = A.alloc([128, KC, H4], BF16, "wgb")
        gpc = A.alloc([H4, 8], F32, "gpc")
        Yx = A.alloc([H4, SEG], F32, "Yx")
        Ye = A.alloc([H4, SEG], F32, "Ye")
        Yy = A.alloc([H4, SEG], F32, "Yy")
        PW = 96
        padA = A.alloc([H4, NCH, PW], F32, "padA")
        padB = A.alloc([H4, NCH, PW], F32, "padB")
        R3 = A.alloc([72, SEG], F32, "R3")
        R5 = A.alloc([8, SEG], F32, "R5")
        Mq = A.alloc([8, SEG], F32, "Mq")
        Me = A.alloc([8, SEG], F32, "Me")
        Mi = A.alloc([8, SEG], F32, "Mi")
        Mn = A.alloc([8, SEG], F32, "Mn")

        def gates_gen():
            k.dma("pool", "wg", wgb, wgb[:, :, :], wg_d, wg_d[s].rearrange("(kc p) c -> p kc c", p=128))
            k.dma("sp", "gp", gpc, gpc[:, :], gp_d, gp_d[s])
            act(gpc, gpc[:, 5:6], gpc, gpc[:, 1:2], AF.Exp)
            yield
            tt("dve", gpc, gpc[:, 5:6], gpc, gpc[:, 5:6], gpc, gpc[:, 3:4], ALU.mult)
            yield
            for tb in range(NTB):
                pf = PF()
                for kc in range(KC):
                    mm(pf, pf[0:H4, 0:TB], wgb, wgb[:, kc, :], nT, nT[:, kc, tb * TB:(tb + 1) * TB],
                       start=(kc == 0), stop=(kc == KC - 1))
                ts("dve", Yx, Yx[:, tb * TB:(tb + 1) * TB], pf, pf[0:H4, 0:TB], gpc[:, 0:1], ALU.add, extra=[gpc])
                yield
            act(Ye, Ye[:, :], Yx, Yx[:, :], AF.Exp, scale=gpc[:, 2:3], extra=[gpc])
            yield
            act(Ye, Ye[:, :], Ye, Ye[:, :], AF.Ln, bias=1.0)
            yield
            ts("dve", Yx, Yx[:, :], Yx, Yx[:, :], gpc[:, 4:5], ALU.mult, extra=[gpc])
            yield
            stt("dve", Yy, Yy[:, :], Ye, Ye[:, :], gpc[:, 5:6], Yx, Yx[:, :], ALU.mult, ALU.add, extra=[gpc])
            yield
            mset("pool", padA, padA[:, :, 0:32], 0.0)
            yield
            mset("pool", padB, padB[:, :, 0:32], 0.0)
            yield
            cp("dve", padA, padA[:, :, 32:96], Yy, Yy[:, :].rearrange("p (c t) -> p c t", t=64))
            yield
            a_, b_ = padA, padB
            for sh in (1, 2, 4, 8, 16, 32):
                tt("dve", b_, b_[:, :, 32:96], a_, a_[:, :, 32:96], a_, a_[:, :, 32 - sh:96 - sh], ALU.add)
                yield
                a_, b_ = b_, a_
            Cs = a_
            for tb in range(NTB):
                c0 = tb * (TB // 64)
                cs_ap = Cs[:, c0:c0 + TB // 64, 32:96]
                pf = PF()
                mm(pf, pf[0:72, 0:TB], cst, cst[0:H4, cfg.c_my1:cfg.c_my1 + 72], Yy, Yy[:, tb * TB:(tb + 1) * TB],
                   start=True, stop=False)
                mm(pf, pf[0:72, 0:TB], cst, cst[0:H4, cfg.c_mc1:cfg.c_mc1 + 72], Cs, cs_ap, start=False, stop=True)
                cp("act", R1, R1[:, tb * TB:(tb + 1) * TB], pf, pf[0:72, 0:TB])
                yield
                pf2 = PF()
                mm(pf2, pf2[0:H, 0:TB], cst, cst[0:H4, cfg.c_mc2:cfg.c_mc2 + H], Cs, cs_ap, start=True, stop=True)
                cp("act", Mb, Mb[0:H, tb * TB:(tb + 1) * TB], pf2, pf2[0:H, 0:TB])
                yield
                pf3 = PF()
                mm(pf3, pf3[0:H, 0:TB], cst, cst[0:H4, cfg.c_my2:cfg.c_my2 + H], Yy, Yy[:, tb * TB:(tb + 1) * TB],
                   start=True, stop=False)
                mm(pf3, pf3[0:H, 0:TB], cst, cst[0:H4, cfg.c_mc2 + 32:cfg.c_mc2 + 32 + H], Cs, cs_ap,
                   start=False, stop=True)
                cp("act", Mq, Mq[0:H, tb * TB:(tb + 1) * TB], pf3, pf3[0:H, 0:TB])
                yield
            c3 = lambda t_, r0=0: t_[r0:r0 + H, :].rearrange("p (c t) -> p c t", t=64)
            cp("dve", chs, chs[0:H, 0, :], R1, c3(R1)[:, :, 63])
            yield
            act(chs, chs[0:H, 1, :], chs, chs[0:H, 0, :], AF.Exp)
            yield
            act(R3, R3[0:H, :], R1, R1[0:H, :], AF.Exp)
            yield
            act(R3, R3[32:32 + H, :], R1, R1[32:32 + H, :], AF.Exp)
            yield
            act(R3, R3[64:64 + H, :], R1, R1[64:64 + H, :], AF.Exp)
            yield
            bc_ch = lambda q: chs[0:H, q:q + 1, :].rearrange("p o c -> p c o").to_broadcast([H, NCH, 64])
            tt("dve", R5, c3(R5), chs, bc_ch(0), R1, c3(R1), ALU.subtract)
            yield
            act(R5, R5[0:H, :], R5, R5[0:H, :], AF.Exp)
            yield
            cp("dve", chs, chs[0:H, 2, :], Mb, c3(Mb)[:, :, 63])
            yield
            tt("dve", Me, c3(Me), Mq, c3(Mq), chs, bc_ch(2), ALU.add)
            yield
            k.op("dve", lambda e: e.tensor_reduce(out=chs[0:H, 3, :], in_=c3(Me), axis=AX.X, op=ALU.max),
                 [Me], [chs])
            yield
            for c in range(NCH):
                cc = slice(c, c + 1)
                cp("dve", chs, chs[0:H, 4, cc], m_w, m_w[0:H, 0:1])
                yield
                tt("dve", m_w, m_w[0:H, 2:3], m_w, m_w[0:H, 0:1], chs, chs[0:H, 2, cc], ALU.add)
                yield
                tt("dve", m_w, m_w[0:H, 0:1], m_w, m_w[0:H, 2:3], chs, chs[0:H, 3, cc], ALU.max)
                yield
                tt("dve", chs, chs[0:H, 5, cc], m_w, m_w[0:H, 2:3], m_w, m_w[0:H, 0:1], ALU.subtract)
                yield
                cp("dve", chs, chs[0:H, 6, cc], m_w, m_w[0:H, 0:1])
                yield
            act(chs, chs[0:H, 5, :], chs, chs[0:H, 5, :], AF.Exp)
            yield
            tt("dve", Me, c3(Me), Me, c3(Me), chs, bc_ch(6), ALU.subtract)
            yield
            act(Me, Me[0:H, :], Me, Me[0:H, :], AF.Exp)
            yield
            if own:
                mset("pool", padA, padA[0:H, :, 0:32], NEG)
                yield
                mset("pool", padB, padB[0:H, :, 0:32], NEG)
                yield
                cp("dve", padA, padA[0:H, :, 32:96], Mq, c3(Mq))
                yield
                a_, b_ = padA, padB
                for sh in (1, 2, 4, 8, 16, 32):
                    tt("dve", b_, b_[0:H, :, 32:96], a_, a_[0:H, :, 32:96], a_, a_[0:H, :, 32 - sh:96 - sh], ALU.max)
                    yield
                    a_, b_ = b_, a_
                tt("dve", Mi, c3(Mi), Mb, c3(Mb), a_, a_[0:H, :, 32:96], ALU.add)
                yield
                tt("dve", Mn, c3(Mn), Mb, c3(Mb), chs, bc_ch(4), ALU.add)
                yield
                tt("dve", Mi, Mi[0:H, :], Mi, Mi[0:H, :], Mn, Mn[0:H, :], ALU.max)
                yield
                tt("dve", Mn, Mn[0:H, :], Mn, Mn[0:H, :], Mi, Mi[0:H, :], ALU.subtract)
                yield
                act(Mn, Mn[0:H, :], Mn, Mn[0:H, :], AF.Exp)
                yield
                tt("dve", Mb, Mb[0:H, :], Mb, Mb[0:H, :], Mi, Mi[0:H, :], ALU.subtract)
                yield
                act(Mi, Mi[0:H, :], Mi, Mi[0:H, :], AF.Exp, scale=-1.0)
                yield
            qsrc = [(TQ_NGC, R1, 0), (TQ_BG, R3, 32), (TQ_KDS, R5, 0), (TQ_EGC, R3, 0), (TQ_BETA, R3, 64),
                    (TQ_DQ, Mq, 0), (TQ_KES, Me, 0)]
            if own:
                qsrc += [(TQ_WINT, Mn, 0), (TQ_EMT, Mi, 0)]
            for p in range(NP):
                pf = PF()
                for (qi, rt, b0) in qsrc:
                    mm(pf, pf[:, qi * H:(qi + 1) * H], rt, rt[b0:b0 + H, p * 128:(p + 1) * 128], cst, idH(b0), bp=b0)
                nq = NQT if own else 7
                cp("act", tok, tok[:, p, 0:nq * H], pf, pf[:, 0:nq * H])
                yield
                ts("dve", tok, tok[:, p, 0:H], tok, tok[:, p, 0:H], -1.0, ALU.mult)
                yield

            b_done[0] = True
            yield

        def prep_gen(br, g, hd):
            kT, qT, ktm, vtm, chb, cvw = hd["kT"], hd["qT"], hd["ktm"], hd["vtm"], hd["chb"], hd["cvw"]
            if br == 0:
                kinds = [("k", cfg.o_gqkv + GK + g * WB, H + g * HG), ("v", cfg.o_gqkv + 2 * GK + g * WB, 2 * H + g * HG)]
                if own:
                    kinds.append(("q", cfg.o_gqkv + g * WB, g * HG))
                cv_d = cvg_d
            else:
                kinds = [("k", cfg.o_mqk + GK + g * WB, H + g * HG), ("v", cfg.o_mv + g * WB, None)]
                if own:
                    kinds.append(("q", cfg.o_mqk + g * WB, g * HG))
                cv_d = cvm_d
            if FLIP:
                kinds = []
                for hh in range(HG):
                    hs_ = br * H + g * HG + hh
                    fk, fv, fq = fkb[hh % 2], fvb[hh % 2], fqb[hh % 2]
                    k.dma("sp", "flk%d" % (hh % 2), fk, fk[:, :, :], sk_d, sk_d[hs_].rearrange("p (n c) -> p n c", c=128))
                    k.dma("sp", "flv%d" % (hh % 2), fv, fv[:, :, :], sv_d, sv_d[hs_].rearrange("p (n c) -> p n c", c=128))
                    k.dma("sp", "flq%d" % (hh % 2), fq, fq[:, :], sq_d, sq_d[hs_])
                    for p0 in range(0, NP, 4):
                        pn = min(4, NP - p0)
                        pf = PF()
                        for j in range(pn):
                            mm(pf, pf[:, j * 128:(j + 1) * 128], Jb, Jb[:, :], fk, fk[:, NP - 1 - (p0 + j), :])
                        cp("act", ktm[hh], ktm[hh][:, p0:p0 + pn, :], pf, pf[:, 0:pn * 128].rearrange("p (n c) -> p n c", c=128))
                        pf = PF()
                        for j in range(pn):
                            mm(pf, pf[:, j * 128:(j + 1) * 128], fk, fk[:, NP - 1 - (p0 + j), :], Jb, Jb[:, :])
                        cp("dve", kT[hh], kT[hh][:, p0 * 128:(p0 + pn) * 128], pf, pf[:, 0:pn * 128])
                        yield
                        pf = PF()
                        for j in range(pn):
                            mm(pf, pf[:, j * 128:(j + 1) * 128], Jb, Jb[:, :], fv, fv[:, NP - 1 - (p0 + j), :])
                        cp("act", vtm[hh], vtm[hh][:, p0:p0 + pn, 0:128], pf, pf[:, 0:pn * 128].rearrange("p (n c) -> p n c", c=128))
                        yield
                    for p0 in range(0, NP, 8):
                        pn = min(8, NP - p0)
                        pb = PB()
                        for j in range(pn):
                            tr(pb, pb[:, j * 128:(j + 1) * 128], fq, fq[:, (p0 + j) * 128:(p0 + j + 1) * 128], identb, identb[:, :])
                        cp("act", qtm, qtm[:, p0:p0 + pn, :], pb, pb[:, 0:pn * 128].rearrange("p (k t) -> p k t", t=128))
                        yield
                    for p0 in range(0, NP, 4):
                        pn = min(4, NP - p0)
                        pf = PF()
                        for j in range(pn):
                            mm(pf, pf[:, j * 128:(j + 1) * 128], qtm, qtm[:, NP - 1 - (p0 + j), :], Jb, Jb[:, :])
                        cp("dve", qT[hh], qT[hh][:, p0 * 128:(p0 + pn) * 128], pf, pf[:, 0:pn * 128])
                        yield
            for ki, (kind, col0, cg0) in enumerate(kinds):
                if cg0 is not None:
                    k.dma("sp", "cv%d" % hd["idx"], cvw, cvw[:, ki * HG:(ki + 1) * HG, :], cv_d, cv_d[s, :, cg0:cg0 + HG, :])
            for ki, (kind, col0, cg0) in enumerate(kinds):
                for hh in range(HG):
                    if hh % HB == 0:
                        wctr[0] += 1
                        wbk = wblk[wctr[0] % 2]
                        k.dma("pool", "wb%d" % (wctr[0] % 2), wbk, wbk[:, :, :], win_d,
                              win_d[:, col0 + hh * 128:col0 + (hh + HB) * 128].rearrange("(kc p) c -> p kc c", p=128))
                    hw_ = hh % HB
                    pcb = pc[hh % 2]
                    pob = post[hh % 2]
                    noconv = (br == 1 and kind == "v")
                    for tb in range(NTB):
                        pf = PF()
                        for kc in range(KC):
                            mm(pf, pf[:, 0:TB], wbk, wbk[:, kc, hw_ * 128:(hw_ + 1) * 128],
                               nT, nT[:, kc, tb * TB:(tb + 1) * TB], start=(kc == 0), stop=(kc == KC - 1))
                        if noconv:
                            cp("act", vT, vT[:, tb * TB:(tb + 1) * TB], pf, pf[:, 0:TB])
                        else:
                            cp("act", pcb, pcb[:, 2 + tb * TB: 2 + (tb + 1) * TB], pf, pf[:, 0:TB])
                        yield
                    if noconv:
                        src_bf = vT
                    else:
                        pf = PF()
                        for kc in range(KC):
                            mm(pf, pf[:, 0:32], wbk, wbk[:, kc, hw_ * 128:(hw_ + 1) * 128], nTh, nTh[:, kc, :],
                               start=(kc == 0), stop=(kc == KC - 1))
                        cp("act", pcb, pcb[:, 0:2], pf, pf[:, 4 * s:4 * s + 2])
                        cp("act", pcb, pcb[:, SEG + 2:SEG + 4], pf, pf[:, 4 * s + 2:4 * s + 4])
                        cw = lambda t_: cvw[:, ki * HG + hh, t_:t_ + 1]
                        for t_ in range(5):
                            ts("dve", dg, dg[:, t_, :], identb, identb[:, :], cw(t_), ALU.mult, extra=[cvw])
                        for tb in range(NTB):
                            pf = PF()
                            for t_ in range(5):
                                mm(pf, pf[:, 0:TB], dg, dg[:, t_, :], pcb, pcb[:, t_ + tb * TB: t_ + (tb + 1) * TB],
                                   start=(t_ == 0), stop=(t_ == 4))
                            act(pob, pob[:, tb * TB:(tb + 1) * TB], pf, pf[:, 0:TB], AF.Silu)
                            yield
                        dst = {"k": kT[hh], "v": vT, "q": qT[hh] if own else None}[kind]
                        if br == 0 and kind in ("k", "q"):
                            act(sqb, sqb[:, :], pob, pob[:, :], AF.Square)
                            for tb in range(NTB):
                                pf = PF()
                                sl = slice(tb * TB, (tb + 1) * TB)
                                mm(pf, pf[:, 0:TB], onesb, onesb[:, :], sqb, sqb[:, sl])
                                act(rinv, rinv[:, sl], pf, pf[:, 0:TB], AF.Ln, bias=float(NORM_EPS))
                                act(rinv, rinv[:, sl], rinv, rinv[:, sl], AF.Exp, scale=-0.5)
                                yield
                            sc = 1.0 if kind == "k" else 128.0 ** -0.5
                            stt("dve", dst, dst[:, :], pob, pob[:, :], sc, rinv, rinv[:, :], ALU.mult, ALU.mult)
                        elif br == 1 and kind == "k":
                            ts("dve", dst, dst[:, :], pob, pob[:, :], 128.0 ** -0.5, ALU.mult)
                        else:
                            cp("dve", dst, dst[:, :], pob, pob[:, :])
                        src_bf = dst
                    if kind in ("k", "v"):
                        dtm = ktm[hh] if kind == "k" else vtm[hh]
                        for p0 in range(0, NP, 8):
                            pn = min(8, NP - p0)
                            pb = PB()
                            for j in range(pn):
                                tr(pb, pb[:, j * 128:(j + 1) * 128], src_bf,
                                   src_bf[:, (p0 + j) * 128:(p0 + j + 1) * 128], identb, identb[:, :])
                            cp("act", dtm, dtm[:, p0:p0 + pn, 0:128],
                               pb, pb[:, 0:pn * 128].rearrange("p (k t) -> p k t", t=128))
                            yield
            if s == 3 and not os.environ.get("MK_NOFLIP"):
                for hh in range(HG):
                    hs_ = br * H + g * HG + hh
                    k.dma("sp", "stk%d" % hs_, sk_d, sk_d[hs_].rearrange("p (n c) -> p n c", c=128), ktm[hh], ktm[hh][:, :, :])
                    k.dma("sp", "stv%d" % hs_, sv_d, sv_d[hs_].rearrange("p (n c) -> p n c", c=128), vtm[hh], vtm[hh][:, :, 0:128])
                    k.dma("sp", "stq%d" % hs_, sq_d, sq_d[hs_], qT[hh], qT[hh][:, :])
            while not b_done[0]:
                yield
            for hh in range(HG):
                h = g * HG + hh
                pf = PF()
                if br == 0:
                    mm(pf, pf[:, 0:NCH], cst, selh(h), chs, chs[0:H, 1, :])
                    cp("act", chb[hh], chb[hh][:, 0, :], pf, pf[:, 0:NCH])
                else:
                    mm(pf, pf[:, 0:NCH], cst, selh(h), chs, chs[0:H, 5, :])
                    cp("act", chb[hh], chb[hh][:, 0, :], pf, pf[:, 0:NCH])

            yield

        def scan_gens(br, g, hd):
            kT, qT, ktm, vtm, chb = hd["kT"], hd["qT"], hd["ktm"], hd["vtm"], hd["chb"]

            prod_done = [0] * HG
            cons_done = [0] * HG

            def prod_gdn(hh):
                h = g * HG + hh
                d = wk[hh]
                B = psf[hh]
                for p in range(NP):
                    b_ = p % 2
                    while p >= cons_done[hh] + 2:
                        yield
                    psl = slice(p * 128, (p + 1) * 128)
                    mm(B, B[:, 0:128], kT[hh], kT[hh][:, psl], kT[hh], kT[hh][:, psl])
                    if own:
                        mm(B, B[:, 128:256], kT[hh], kT[hh][:, psl], qT[hh], qT[hh][:, psl])
                    mm(B, B[:, 256:384], cst, selh(h, 32), R1, R1[32:32 + H, psl], bp=32)
                    if own:
                        mm(B, B[:, 384:512], cst, selh(h), R1, R1[0:H, psl])
                    ts("pool", d["kd"][b_], d["kd"][b_][:, 0:128], ktm[hh], ktm[hh][:, p, :], tokc(p, TQ_KDS, h), ALU.mult, extra=[tok])
                    ts("pool", d["vbb"], d["vbb"][:, 0:128], vtm[hh], vtm[hh][:, p, 0:128], tokc(p, TQ_BETA, h), ALU.mult, extra=[tok])
                    ts("dve", d["kbg"], d["kbg"][:, 0:128], ktm[hh], ktm[hh][:, p, :], tokc(p, TQ_BG, h), ALU.mult, extra=[tok])
                    yield
                    stt("dve", d["eu"], d["eu"][:, :], B, B[:, 256:384], tokc(p, TQ_NGC, h), cst, mbs(), ALU.add, ALU.add, extra=[tok])
                    yield
                    act(d["eu"], d["eu"][:, :], d["eu"], d["eu"][:, :], AF.Exp)
                    yield
                    tt("dve", d["UL"], d["UL"][:, 0, 0:128], d["eu"], d["eu"][:, :], B, B[:, 0:128], ALU.mult)
                    yield
                    if own:
                        stt("dve", d["gm"], d["gm"][:, :], B, B[:, 384:512], tokc(p, TQ_NGC, h), cst, mbi(), ALU.add, ALU.add, extra=[tok])
                        yield
                        act(d["gm"], d["gm"][:, :], d["gm"], d["gm"][:, :], AF.Exp)
                        yield
                        tt("dve", d["AT"][b_], d["AT"][b_][:, 0:128], d["gm"], d["gm"][:, :], B, B[:, 128:256], ALU.mult)
                    ULc, ULn = d["UL"], d["UL2"]
                    pb = psb[0]
                    po_ = hh * 128
                    tr(pb, pb[:, po_:po_ + 128], ULc, ULc[:, 0, 0:128], identb, identb[:, :])
                    tt("dve", d["P"], d["P"][:, 0:128], identb, identb[:, :], ULc, ULc[:, 0, 0:128], ALU.subtract)
                    yield
                    cp("act", ULc, ULc[:, 1, 0:128], pb, pb[:, po_:po_ + 128])
                    yield
                    Pc, Pn = d["P"], d["P2"]
                    U_ = lambda t_: t_[:, 0, 0:128]
                    L_ = lambda t_: t_[:, 1, 0:128]
                    mm(B, B[:, 0:128], ULc, L_(ULc), ULc, U_(ULc))
                    mm(B, B[:, 128:256], ULc, U_(ULc), ULc, L_(ULc))
                    yield
                    cp("act", ULn, ULn[:, :, 0:128], B, B[:, 0:256].rearrange("p (n c) -> p n c", c=128))
                    yield
                    ULc, ULn = ULn, ULc
                    for lvl in range(4):
                        last = (lvl == 3)
                        if not last:
                            mm(B, B[:, 0:128], ULc, L_(ULc), ULc, U_(ULc))
                        mm(B, B[:, 128:256], ULc, U_(ULc), ULc, L_(ULc))
                        mm(B, B[:, 256:384], ULc, L_(ULc), Pc, Pc[:, 0:128])
                        yield
                        if not last:
                            cp("act", ULn, ULn[:, :, 0:128], B, B[:, 0:256].rearrange("p (n c) -> p n c", c=128))
                        else:
                            cp("act", ULn, ULn[:, 1, 0:128], B, B[:, 128:256])
                        tt("dve", Pn, Pn[:, 0:128], B, B[:, 256:384], Pc, Pc[:, 0:128], ALU.add)
                        yield
                        ULc, ULn = ULn, ULc
                        Pc, Pn = Pn, Pc
                    Lc = ULc
                    mm(B, B[:, 0:128], Lc, L_(Lc), Pc, Pc[:, 0:128])
                    yield
                    tt("dve", Pn, Pn[:, 0:128], B, B[:, 0:128], Pc, Pc[:, 0:128], ALU.add)
                    TT = Pn
                    yield
                    mm(B, B[:, 0:128], TT, TT[:, 0:128], d["vbb"], d["vbb"][:, 0:128])
                    mm(B, B[:, 128:256], d["kbg"], d["kbg"][:, 0:128], TT, TT[:, 0:128])
                    yield
                    cp("act", d["ub"][b_], d["ub"][b_][:, :], B, B[:, 0:128])
                    cp("dve", d["wT"][b_], d["wT"][b_][:, 0:128], B, B[:, 128:256])
                    prod_done[hh] = p + 1
                    yield

            def cons_gdn(hh):
                h = g * HG + hh
                hs = br * H + h
                d = wk[hh]
                Sw, Sb = st_w[hs], st_b[hs]
                C = psf[4 + hh // 2]
                c0 = (hh % 2) * 256
                for p in range(NP):
                    b_ = p % 2
                    psl = slice(p * 128, (p + 1) * 128)
                    while prod_done[hh] <= p:
                        yield
                    for ci in range(2):
                        c = 2 * p + ci
                        rs = slice(ci * 64, ci * 64 + 64)
                        mm(C, C[:, c0:c0 + 128], d["wT"][b_], d["wT"][b_][:, 0:128], Sb, Sb[:, 0:128])
                        if own:
                            mm(C, C[:, c0 + 128:c0 + 256], qT[hh], qT[hh][:, psl], Sb, Sb[:, 0:128])
                        yield
                        tt("dve", d["vnew"], d["vnew"][rs, 0:128], d["ub"][b_], d["ub"][b_][rs, :], C, C[rs, c0:c0 + 128], ALU.subtract)
                        if own:
                            act(d["t1"], d["t1"][rs, 0:128], C, C[rs, c0 + 128:c0 + 256], AF.Copy,
                                scale=tokc(p, TQ_EGC, h, rs.start, rs.stop), extra=[tok])
                        yield
                        mm(C, C[:, c0:c0 + 128], d["kd"][b_], d["kd"][b_][rs, 0:128], d["vnew"], d["vnew"][rs, 0:128], bp=rs.start)
                        if own:
                            mm(C, C[:, c0 + 128:c0 + 256], d["AT"][b_], d["AT"][b_][rs, 0:128], d["vnew"], d["vnew"][rs, 0:128], bp=rs.start)
                        yield
                        stt("dve", Sw, Sw[:, 0:128], Sw, Sw[:, 0:128], chb[hh][:, 0, c:c + 1], C, C[:, c0:c0 + 128],
                            ALU.mult, ALU.add, extra=[chb[hh]])
                        if own:
                            tt("dve", d["o"], d["o"][rs, :], C, C[rs, c0 + 128:c0 + 256], d["t1"], d["t1"][rs, 0:128], ALU.add)
                        yield
                        cp("act", Sb, Sb[:, 0:128], Sw, Sw[:, 0:128])
                        yield
                    if own:
                        od = of_d[s - 3]
                        k.dma("sp", "ost%d" % hh, od, od[psl, hs * 128:(hs + 1) * 128], d["o"], d["o"][:, :])
                    cons_done[hh] = p + 1

            def scan_mlstm(hh):
                h = g * HG + hh
                hs = br * H + h
                d = wk[hh]
                Sw, Sb = st_w[hs], st_b[hs]
                for p in range(NP):
                    psl = slice(p * 128, (p + 1) * 128)
                    if own:
                        pg = psf[hh]
                        mm(pg, pg[:, 0:128], kT[hh], kT[hh][:, psl], qT[hh], qT[hh][:, psl])
                        mm(pg, pg[:, 128:256], cst, selh(h), Mb, Mb[0:H, psl])
                    act(d["kd"][0], d["kd"][0][:, 0:128], ktm[hh], ktm[hh][:, p, :], AF.Copy, scale=tokc(p, TQ_KES, h), extra=[tok])
                    yield
                    if own:
                        stt("dve", d["eu"], d["eu"][:, :], pg, pg[:, 128:256], tokc(p, TQ_DQ, h), cst, mbi(), ALU.add, ALU.add, extra=[tok])
                        yield
                        act(d["eu"], d["eu"][:, :], d["eu"], d["eu"][:, :], AF.Exp)
                        yield
                        tt("dve", d["AT"][0], d["AT"][0][:, 0:128], d["eu"], d["eu"][:, :], pg, pg[:, 0:128], ALU.mult)
                        yield
                    for ci in range(2):
                        c = 2 * p + ci
                        rs = slice(ci * 64, ci * 64 + 64)
                        if own:
                            po = psf[hh]
                            mm(po, po[:, 0:129], qT[hh], qT[hh][:, psl], Sb, Sb[:, 0:129])
                            mm(po, po[:, 130:259], d["AT"][0], d["AT"][0][rs, 0:128], vtm[hh], vtm[hh][rs, p, 0:129], bp=rs.start)
                        pu = psf[hh]
                        mm(pu, pu[:, 260:389], d["kd"][0], d["kd"][0][rs, 0:128], vtm[hh], vtm[hh][rs, p, 0:129], bp=rs.start)
                        yield
                        if own:
                            act(d["t1"], d["t1"][rs, 0:129], po, po[rs, 0:129], AF.Copy,
                                scale=tokc(p, TQ_WINT, h, rs.start, rs.stop), extra=[tok])
                            yield
                            tt("dve", d["t1"], d["t1"][rs, 0:129], po, po[rs, 130:259], d["t1"], d["t1"][rs, 0:129], ALU.add)
                            yield
                        stt("dve", Sw, Sw[:, 0:129], Sw, Sw[:, 0:129], chb[hh][:, 0, c:c + 1], pu, pu[:, 260:389],
                            ALU.mult, ALU.add, extra=[chb[hh]])
                        yield
                        cp("act", Sb, Sb[:, 0:129], Sw, Sw[:, 0:129])
                        yield
                    if own:
                        ts("dve", d["dm"], d["dm"][:, 0:1], d["t1"], d["t1"][:, 128:129], -1.0, ALU.mult,
                           d["t1"][:, 128:129], ALU.max)
                        yield
                        ts("dve", d["dm"], d["dm"][:, 0:1], d["dm"], d["dm"][:, 0:1], tokc(p, TQ_EMT, h), ALU.max, extra=[tok])
                        yield
                        k.op("dve", lambda e: e.reciprocal(out=d["dm"][:, 1:2], in_=d["dm"][:, 0:1]), [d["dm"]], [d["dm"]])
                        yield
                        ts("dve", d["o"], d["o"][:, :], d["t1"], d["t1"][:, 0:128], d["dm"][:, 1:2], ALU.mult, extra=[d["dm"]])
                        yield
                        od = of_d[s - 3]
                        k.dma("sp", "ost%d" % hh, od, od[psl, hs * 128:(hs + 1) * 128], d["o"], d["o"][:, :])


            if br == 0:
                return [prod_gdn(hh) for hh in range(HG)] + [cons_gdn(hh) for hh in range(HG)]
            return [scan_mlstm(hh) for hh in range(HG)]

        groups = [(br_, g_) for br_ in range(2) for g_ in range(NG)]
        STAG = int(os.environ.get("MK_STAG", "3"))
        CDEL = int(os.environ.get("MK_CDEL", "24"))
        BPER = int(os.environ.get("MK_BPER", "4"))
        bgn = gates_gen()
        pgn = prep_gen(groups[0][0], groups[0][1], HD[0])
        while bgn is not None or pgn is not None:
            if bgn is not None:
                try:
                    for _ in range(BPER):
                        next(bgn)
                except StopIteration:
                    bgn = None
            if pgn is not None:
                try:
                    next(pgn)
                except StopIteration:
                    pgn = None
        k.barrier()
        k.mark("B%d" % s)
        A.release(mX)
        alloc_hd()
        wk = []
        for hh in range(HG):
            d = {}
            d["eu"] = A.alloc([128, 128], F32, "eu")
            d["gm"] = A.alloc([128, 128], F32, "gm")
            for nm in ("P", "P2", "vnew", "vbb", "kbg"):
                d[nm] = A.alloc([128, 132], BF16, nm)
            d["UL"] = A.alloc([128, 2, 132], BF16, "UL")
            d["UL2"] = A.alloc([128, 2, 132], BF16, "UL2")
            for nm in ("AT", "kd", "wT"):
                d[nm] = [A.alloc([128, 132], BF16, nm) for _ in range(2)]
            d["ub"] = [A.alloc([128, 128], F32, "ub") for _ in range(2)]
            d["t1"] = A.alloc([128, 132], F32, "t1")
            d["t2"] = A.alloc([128, 132], F32, "t2")
            d["o"] = A.alloc([128, 128], F32, "o")
            d["dm"] = A.alloc([128, 4], F32, "dm")
            wk.append(d)
        for gi, (br_, g_) in enumerate(groups):
            if gi > 0:
                for _ in prep_gen(br_, g_, HD[gi % 2]):
                    pass
            gens = scan_gens(br_, g_, HD[gi % 2])
            offs = [(i % HG) * STAG + (CDEL if i >= HG else 0) for i in range(len(gens))]
            live = list(range(len(gens)))
            tstep = 0
            while live:
                for i in list(live):
                    if tstep < offs[i]:
                        continue
                    try:
                        next(gens[i])
                    except StopIteration:
                        live.remove(i)
                tstep += 1
            k.mark("C%d%d%d" % (s, br_, g_))
        rr["fpool"], rr["bpool"] = list(range(6)), [0, 1]
        k.barrier()
        A.release(mC)
        if os.environ.get("MK_STOP") == "C%d" % s:
            return nc

        fl = lambda j: flags[:, j:j + 1]
        flh = lambda j: flags[0:H, j:j + 1]
        if s in (0, 1):
            for hs in range(2 * H):
                stt("dve", st_f[hs], st_f[hs][:, :], st_w[hs], st_w[hs][:, :], fl(s), st_f[hs], st_f[hs][:, :],
                    ALU.mult, ALU.add, extra=[flags])
                ts("dve", st_w[hs], st_w[hs][:, :], st_w[hs], st_w[hs][:, :], fl(2 + s), ALU.mult, extra=[flags])
                cp("act", st_b[hs], st_b[hs][:, :], st_w[hs], st_w[hs][:, :])
            stt("dve", m_w, m_w[0:H, 1:2], m_w, m_w[0:H, 0:1], flh(s), m_w, m_w[0:H, 1:2], ALU.mult, ALU.add, extra=[flags])
            ts("dve", m_w, m_w[0:H, 0:1], m_w, m_w[0:H, 0:1], flh(2 + s), ALU.mult, extra=[flags])
        elif s == 2:
            for hs in range(2 * H):
                stt("dve", st_f[hs], st_f[hs][:, :], st_w[hs], st_w[hs][:, :], fl(4), st_f[hs], st_f[hs][:, :],
                    ALU.mult, ALU.add, extra=[flags])
                ts("dve", st_w[hs], st_w[hs][:, :], st_w[hs], st_w[hs][:, :], fl(5), ALU.mult, extra=[flags])
            tmp = A.alloc([128, SW], F32, "swp")
            for hs in range(2 * H):
                cp("dve", tmp, tmp[:, :], st_w[hs], st_w[hs][:, :])
                cp("dve", st_w[hs], st_w[hs][:, :], st_f[hs], st_f[hs][:, :])
                cp("dve", st_f[hs], st_f[hs][:, :], tmp, tmp[:, :])
                cp("act", st_b[hs], st_b[hs][:, :], st_w[hs], st_w[hs][:, :])
            stt("dve", m_w, m_w[0:H, 1:2], m_w, m_w[0:H, 0:1], flh(4), m_w, m_w[0:H, 1:2], ALU.mult, ALU.add, extra=[flags])
            ts("dve", m_w, m_w[0:H, 0:1], m_w, m_w[0:H, 0:1], flh(5), ALU.mult, extra=[flags])
            cp("dve", m_w, m_w[0:H, 2:3], m_w, m_w[0:H, 0:1])
            cp("dve", m_w, m_w[0:H, 0:1], m_w, m_w[0:H, 1:2])
            cp("dve", m_w, m_w[0:H, 1:2], m_w, m_w[0:H, 2:3])
        elif s == 3:
            for hs in range(2 * H):
                cp("dve", st_w[hs], st_w[hs][:, :], st_f[hs], st_f[hs][:, :])
                cp("act", st_b[hs], st_b[hs][:, :], st_w[hs], st_w[hs][:, :])
            cp("dve", m_w, m_w[0:H, 0:1], m_w, m_w[0:H, 1:2])
        k.barrier()

    if os.environ.get("MK_STOP") == "S":
        return nc
    A.release(mark0)
    k.barrier()
    yT = A.alloc([128, 2 * H, SEG], BF16, "yT")
    m_top = A.mark()
    nT = A.alloc([128, KC, SEG], BF16, "nTo")
    mE = A.mark()
    xbo = [A.alloc([128, D], F32, "xbo") for _ in range(2)]
    scr = [(A.alloc([128, D], BF16, "junk"), A.alloc([128, 4], F32, "ssb"), A.alloc([128, D], BF16, "xn"))
           for _ in range(2)]
    nwr = A.alloc([128, D], F32, "nwr")
    k.dma("sp", "c_nwr", nwr, nwr[:, :], nwr_d, nwr_d[:, :])
    for p in range(NP):
        xb = xbo[p % 2]
        k.dma("sp", "xl%d" % (p % 2), xb, xb[:, :], xs_d, xs_d[3, p * 128:(p + 1) * 128, :])

        def dst_o(k0, kn, pb, p=p):
            cp("act" if (k0 // 8) % 2 == 0 else "dve", nT, nT[:, k0:k0 + kn, p * 128:(p + 1) * 128],
               pb, pb[:, 0:kn * 128].rearrange("p (k t) -> p k t", t=128))
        rms_to_nT(xb, 128, dst_o, nwr, scr[p % 2])
    k.barrier()
    A.release(mE)

    k.mark("E0")
    CB = min(512, GK)
    wtm = [A.alloc([128, KC, CB], BF16, "wtm") for _ in range(2)]
    gbr = A.alloc([128, 2 * D], F32, "gbr")
    k.dma("sp", "c_gbr", gbr, gbr[:, :], gb_d, gb_d[:, :])
    evb = [A.alloc([128, CB], F32, "evb") for _ in range(3)]
    blocks = []
    for c0 in range(0, GK, CB):
        blocks.append((cfg.o_gz + c0, 0 * GK + c0, "silu", None))
    for c0 in range(0, GK, CB):
        blocks.append((cfg.o_mo + c0, 1 * GK + c0, "sig", None))
    for c0 in range(0, GK, CB):
        blocks.append((cfg.o_mz + c0, 2 * GK + c0, "silu", None))
    n_a = len(blocks)
    for c0 in range(0, 2 * D, CB):
        blocks.append((cfg.o_gates + c0, 3 * GK + c0, "sig", c0))
    eic = [0]

    def e1_gen(lo, hi):
        for bi in range(lo, hi):
            wc0, pc0, fn, gb0 = blocks[bi]
            wt = wtm[bi % 2]
            k.dma("pool", "wt%d" % (bi % 2), wt, wt[:, :, :], win_d,
                  win_d[:, wc0:wc0 + CB].rearrange("(kc p) c -> p kc c", p=128))
            for p in range(NP):
                pf = PF()
                for kc in range(KC):
                    mm(pf, pf[:, 0:CB], nT, nT[:, kc, p * 128:(p + 1) * 128], wt, wt[:, kc, :],
                       start=(kc == 0), stop=(kc == KC - 1))
                ev = evb[eic[0] % 3]
                eic[0] += 1
                if gb0 is not None:
                    tt("dve", ev, ev[:, :], pf, pf[:, 0:CB], gbr, gbr[:, gb0:gb0 + CB], ALU.add)
                    act(ev, ev[:, :], ev, ev[:, :], AF.Sigmoid)
                else:
                    act(ev, ev[:, :], pf, pf[:, 0:CB], AF.Silu if fn == "silu" else AF.Sigmoid)
                k.dma("sp", "pst%d" % ((eic[0] - 1) % 3), ptm_d, ptm_d[p * 128:(p + 1) * 128, pc0:pc0 + CB], ev, ev[:, :])
                yield

    for _ in e1_gen(0, n_a):
        pass
    k.mark("E1")
    gnw = A.alloc([128, GK], F32, "gnw")
    mnw = A.alloc([128, GK], F32, "mnw")
    k.dma("sp", "c_gnw", gnw, gnw[:, :], gnw_d, gnw_d[:, :])
    k.dma("sp", "c_mnw", mnw, mnw[:, :], mnw_d, mnw_d[:, :])
    Jf = lambda: cst[:, cfg.c_J:cfg.c_J + 128]
    ofb = A.alloc([128, 2 * GK], F32, "ofb")
    obb = A.alloc([128, 2 * GK], F32, "obb")
    zb = A.alloc([128, 3 * GK], F32, "zb")
    st4 = A.alloc([128, 4, 2 * H], F32, "st4")
    yb = A.alloc([128, 2 * GK], BF16, "yb")
    ptm_z = T(ptm_d.h, "ptm_z")
    ptm_z.w = ptm_d.w

    def e2_gen():
        for p in range(NP):
            of_, ob_, z_ = ofb, obb, zb
            k.dma("pool", "ld_of", of_, of_[:, :], of_d[0], of_d[0][p * 128:(p + 1) * 128, :])
            k.dma("pool", "ld_ob", ob_, ob_[:, :], of_d[1], of_d[1][(NP - 1 - p) * 128:(NP - p) * 128, :])
            k.dma("pool", "ld_z", z_, z_[:, :], ptm_z, ptm_d[p * 128:(p + 1) * 128, 0:3 * GK])
            yield
            for c0 in range(0, 2 * GK, 512):
                cn = min(512, 2 * GK - c0)
                pf = PF()
                mm(pf, pf[:, 0:cn], cst, Jf(), ob_, ob_[:, c0:c0 + cn])
                tt("dve", of_, of_[:, c0:c0 + cn], pf, pf[:, 0:cn], of_, of_[:, c0:c0 + cn], ALU.add)
                yield
            o3 = of_[:, :].rearrange("p (h d) -> p h d", d=128)
            sq = ob_
            act(sq, sq[:, :], of_, of_[:, :], AF.Square)
            yield
            k.op("dve", lambda e: e.tensor_reduce(out=st4[:, 0, :], in_=sq[:, :].rearrange("p (h d) -> p h d", d=128),
                                                 axis=AX.X, op=ALU.add), [sq], [st4])
            yield
            k.op("dve", lambda e: e.tensor_reduce(out=st4[:, 1, :], in_=o3, axis=AX.X, op=ALU.add), [of_], [st4])
            ts("dve", st4, st4[:, 1, 0:H], st4, st4[:, 1, 0:H], 0.0, ALU.mult)
            ts("dve", st4, st4[:, 1, :], st4, st4[:, 1, :], 1.0 / 128.0, ALU.mult)
            tt("dve", st4, st4[:, 2, :], st4, st4[:, 1, :], st4, st4[:, 1, :], ALU.mult)
            stt("dve", st4, st4[:, 2, :], st4, st4[:, 0, :], 1.0 / 128.0, st4, st4[:, 2, :], ALU.mult, ALU.subtract)
            ts("dve", st4, st4[:, 2, :], st4, st4[:, 2, :], 0.0, ALU.max)
            act(st4, st4[:, 2, :], st4, st4[:, 2, :], AF.Ln, bias=float(NORM_EPS))
            act(st4, st4[:, 2, :], st4, st4[:, 2, :], AF.Exp, scale=-0.5)
            yield
            tt("dve", of_, o3, of_, o3, st4, st4[:, 1, :].rearrange("p (h o) -> p h o", o=1).to_broadcast([128, 2 * H, 128]),
               ALU.subtract)
            yield
            tt("dve", of_, o3, of_, o3, st4, st4[:, 2, :].rearrange("p (h o) -> p h o", o=1).to_broadcast([128, 2 * H, 128]),
               ALU.mult)
            yield
            tt("dve", of_, of_[:, 0:GK], of_, of_[:, 0:GK], gnw, gnw[:, :], ALU.mult)
            yield
            tt("dve", of_, of_[:, GK:2 * GK], of_, of_[:, GK:2 * GK], mnw, mnw[:, :], ALU.mult)
            yield
            tt("pool", z_, z_[:, GK:2 * GK], z_, z_[:, GK:2 * GK], z_, z_[:, 2 * GK:3 * GK], ALU.mult)
            yield
            tt("dve", yb, yb[:, :], of_, of_[:, :], z_, z_[:, 0:2 * GK], ALU.mult)
            yield
            for k0 in range(0, 2 * H, 8):
                kn = min(8, 2 * H - k0)
                pb = PB()
                for j in range(kn):
                    tr(pb, pb[:, j * 128:(j + 1) * 128], yb, yb[:, (k0 + j) * 128:(k0 + j + 1) * 128], identb, identb[:, :])
                cp("act", yT, yT[:, k0:k0 + kn, p * 128:(p + 1) * 128],
                   pb, pb[:, 0:kn * 128].rearrange("p (k t) -> p k t", t=128))
                yield

    g1 = e1_gen(n_a, len(blocks))
    g2 = e2_gen()
    E2PER = int(os.environ.get("MK_E2PER", "3"))
    while g1 is not None or g2 is not None:
        if g1 is not None:
            try:
                next(g1)
            except StopIteration:
                g1 = None
        if g2 is not None:
            try:
                for _ in range(E2PER):
                    next(g2)
            except StopIteration:
                g2 = None
    k.barrier()
    A.release(m_top)

    k.mark("E2")
    mT = A.alloc([128, KC, SEG], BF16, "mT")
    mE = A.mark()
    DB = min(512, D)
    wab = [A.alloc([128, 2 * H, DB], BF16, "wab")] * 2
    g01 = [A.alloc([128, 2, DB], F32, "g01") for _ in range(2)]
    mg = A.alloc([128, DB], F32, "mg")
    mgb = A.alloc([128, DB], BF16, "mgb")
    gi = 0
    for bi, d0 in enumerate(range(0, D, DB)):
        wt = wab[bi % 2]
        k.dma("pool", "wab%d" % (bi % 2), wt, wt[:, 0:H, :], wa_d,
              wa_d[:, d0:d0 + DB].rearrange("(kc p) c -> p kc c", p=128))
        k.dma("pool", "wab%d" % (bi % 2), wt, wt[:, H:2 * H, :], wb_d,
              wb_d[:, d0:d0 + DB].rearrange("(kc p) c -> p kc c", p=128))
        for p in range(NP):
            gt = g01[gi % 2]
            gi += 1
            k.dma("sp", "ld_g%d" % (gi % 2), gt, gt[:, 0, :], ptm_d, ptm_d[p * 128:(p + 1) * 128, 3 * GK + d0:3 * GK + d0 + DB])
            k.dma("sp", "ld_g%d" % (gi % 2), gt, gt[:, 1, :], ptm_d,
                  ptm_d[p * 128:(p + 1) * 128, 3 * GK + D + d0:3 * GK + D + d0 + DB])
            pfa = PF()
            for kc in range(H):
                mm(pfa, pfa[:, 0:DB], yT, yT[:, kc, p * 128:(p + 1) * 128], wt, wt[:, kc, :], start=(kc == 0), stop=(kc == H - 1))
            pfb = PF()
            for kc in range(H):
                mm(pfb, pfb[:, 0:DB], yT, yT[:, H + kc, p * 128:(p + 1) * 128], wt, wt[:, H + kc, :],
                   start=(kc == 0), stop=(kc == H - 1))
            tt("dve", mg, mg[:, :], pfa, pfa[:, 0:DB], gt, gt[:, 0, :], ALU.mult)
            tt("dve", gt, gt[:, 1, :], pfb, pfb[:, 0:DB], gt, gt[:, 1, :], ALU.mult)
            tt("pool", mgb, mgb[:, :], mg, mg[:, :], gt, gt[:, 1, :], ALU.add)
            pb = PB()
            nk = DB // 128
            for j in range(nk):
                tr(pb, pb[:, j * 128:(j + 1) * 128], mgb, mgb[:, j * 128:(j + 1) * 128], identb, identb[:, :])
            cp("act", mT, mT[:, d0 // 128:d0 // 128 + nk, p * 128:(p + 1) * 128],
               pb, pb[:, 0:nk * 128].rearrange("p (k t) -> p k t", t=128))
    k.barrier()
    A.release(mE)

    k.mark("E3")
    mE = A.mark()
    xo = A.alloc([128, NP, D], F32, "xo")
    for p in range(NP):
        k.dma("sp", "xl%d" % (p % 2), xo, xo[:, p, :], xs_d, xs_d[3, p * 128:(p + 1) * 128, :])
    wob = [A.alloc([128, KC, DB], BF16, "wob") for _ in range(2)]
    for bi, d0 in enumerate(range(0, D, DB)):
        wt = wob[bi % 2]
        k.dma("pool", "wob%d" % (bi % 2), wt, wt[:, :, :], wo_d,
              wo_d[:, d0:d0 + DB].rearrange("(kc p) c -> p kc c", p=128))
        for p in range(NP):
            pf = PF()
            for kc in range(KC):
                mm(pf, pf[:, 0:DB], mT, mT[:, kc, p * 128:(p + 1) * 128], wt, wt[:, kc, :], start=(kc == 0), stop=(kc == KC - 1))
            tt("dve", xo, xo[:, p, d0:d0 + DB], pf, pf[:, 0:DB], xo, xo[:, p, d0:d0 + DB], ALU.add)
    k.barrier()
    fnw = A.alloc([128, D], F32, "fnw")
    k.dma("sp", "c_fnw", fnw, fnw[:, :], fnw_d, fnw_d[:, :])
    junkf = [A.alloc([128, D], BF16, "junkf")] * 2
    ss2 = [A.alloc([128, 4], F32, "ss2") for _ in range(2)]
    xo_t = [T(xo.h, "xo%d" % p) for p in range(NP)]
    out_t = [T(out_d.h, "out%d" % p) for p in range(NP)]
    k.wait_all("dve", [fnw])
    for p in range(NP):
        jf, s2 = junkf[p % 2], ss2[p % 2]
        act(jf, jf[:, :], xo_t[p], xo[:, p, :], AF.Square, accum=s2[:, 0:1], accum_t=s2)
        act(s2, s2[:, 1:2], s2, s2[:, 0:1], AF.Ln, bias=float(NORM_EPS), scale=1.0 / D)
        act(s2, s2[:, 2:3], s2, s2[:, 1:2], AF.Exp, scale=-0.5)
        stt("dve", xo_t[p], xo[:, p, :], xo_t[p], xo[:, p, :], s2[:, 2:3], fnw, fnw[:, :], ALU.mult, ALU.mult, extra=[s2])
        k.dma("sp", "fin%d" % p, out_t[p], out_d[p * 128:(p + 1) * 128, :], xo_t[p], xo[:, p, :])
    k.wait_all("sp", out_t)
    k.barrier()
    k.mark("END")
    build_nc.last_k = k
    return nc


def make_consts(cfg):
    H = cfg.H
    c = np.zeros((128, cfg.NCST), np.float32)
    idx = np.arange(128)
    c[:, cfg.c_ident:cfg.c_ident + 128] = np.eye(128, dtype=np.float32)
    c[:, cfg.c_J:cfg.c_J + 128] = np.eye(128, dtype=np.float32)[::-1]
    same = (idx[:, None] // 64) == (idx[None, :] // 64)
    c[:, cfg.c_tri:cfg.c_tri + 128] = (same & (idx[None, :] >= idx[:, None])).astype(np.float32)
    c[:, cfg.c_tris:cfg.c_tris + 128] = (same & (idx[None, :] > idx[:, None])).astype(np.float32)
    c[:, cfg.c_ones:cfg.c_ones + 128] = 1.0
    c[:, cfg.c_mbs:cfg.c_mbs + 128] = np.where(same & (idx[None, :] > idx[:, None]), 0.0, -30000.0)
    c[:, cfg.c_mbi:cfg.c_mbi + 128] = np.where(same & (idx[None, :] >= idx[:, None]), 0.0, -30000.0)
    for h in range(H):
        c[h, cfg.c_sel + h * 128: cfg.c_sel + (h + 1) * 128] = 1.0
        c[32 + h, cfg.c_sel + h * 128: cfg.c_sel + (h + 1) * 128] = 1.0
        c[64 + h, cfg.c_sel + h * 128: cfg.c_sel + (h + 1) * 128] = 1.0
        c[h, cfg.c_idH + h] = 1.0
        c[32 + h, cfg.c_idH + h] = 1.0
        c[64 + h, cfg.c_idH + h] = 1.0
    for h in range(H):
        c[H + h, cfg.c_mc1 + h] = 1.0
        c[H + h, cfg.c_mc1 + 32 + h] = 1.0
        c[h, cfg.c_my1 + 32 + h] = 1.0
        c[h, cfg.c_my1 + 64 + h] = 1.0
        c[3 * H + h, cfg.c_mc2 + h] = 1.0
        c[2 * H + h, cfg.c_my2 + h] = 1.0
        c[3 * H + h, cfg.c_mc2 + 32 + h] = -1.0
    return c


def prep_core(inp, cfg, b, tq):
    D, H, SEG, GK = cfg.D, cfg.H, cfg.SEG, cfg.GK
    SEQ = 4 * SEG
    x = inp["x"][b]
    w_in = np.ascontiguousarray(inp["w_in"][0])
    slots = []
    for s in range(3):
        slots.append((s, 0) if s < tq else (3 + tq - s, 1))
    slots += [(tq, 0), (tq, 1)]
    xs = np.zeros((5, SEG, D), np.float32)
    xh = np.zeros((32, D), np.float32)
    ext = np.zeros((SEQ + 4, D), np.float32)
    ext[2:SEQ + 2] = x
    wg = np.zeros((5, D, 4 * H), np.float32)
    gp = np.zeros((5, 4 * H, 8), np.float32)
    cvg = np.zeros((5, 128, 3 * H, 5), np.float32)
    cvm = np.zeros((5, 128, 2 * H, 5), np.float32)
    cg = inp["conv_gdn"][0]
    cm = inp["conv_mlstm"][0]
    for s, (seg, dr) in enumerate(slots):
        e = ext[seg * SEG: seg * SEG + SEG + 4]
        if dr == 1:
            e = e[::-1]
        xs[s] = e[2:SEG + 2]
        xh[4 * s: 4 * s + 2] = e[0:2]
        xh[4 * s + 2: 4 * s + 4] = e[SEG + 2: SEG + 4]
        sl = slice(dr * H, (dr + 1) * H)
        wg[s, :, 0:H] = w_in[:, cfg.o_gbeta:cfg.o_gbeta + 2 * H][:, sl]
        wg[s, :, H:2 * H] = w_in[:, cfg.o_ga:cfg.o_ga + 2 * H][:, sl]
        wg[s, :, 2 * H:3 * H] = w_in[:, cfg.o_mi:cfg.o_mi + 2 * H][:, sl]
        wg[s, :, 3 * H:4 * H] = w_in[:, cfg.o_mf:cfg.o_mf + 2 * H][:, sl]
        gp[s, H:2 * H, 0] = inp["gdn_dt_bias"][0, dr]
        gp[s, 2 * H:3 * H, 0] = inp["mlstm_i_bias"][0, dr]
        gp[s, 3 * H:4 * H, 0] = inp["mlstm_f_bias"][0, dr]
        gp[s, H:2 * H, 1] = inp["gdn_a_log"][0, dr]
        gp[s, 0:H, 2] = -1.0
        gp[s, H:2 * H, 2] = 1.0
        gp[s, 3 * H:4 * H, 2] = -1.0
        gp[s, 0:H, 3] = -1.0
        gp[s, H:2 * H, 3] = -1.0
        gp[s, 3 * H:4 * H, 3] = -1.0
        gp[s, 2 * H:3 * H, 4] = 1.0
        cgs = cg[::-1] if dr == 1 else cg
        cms = cm[::-1] if dr == 1 else cm
        cvg[s] = cgs.T.reshape(3 * H, 128, 5).transpose(1, 0, 2)
        cvm[s] = cms.T.reshape(2 * H, 128, 5).transpose(1, 0, 2)
    flags = np.zeros((128, 8), np.float32)
    sw0, sw1, f3 = float(tq == 1), float(tq == 2), float(tq == 3)
    flags[:, 0], flags[:, 1], flags[:, 2], flags[:, 3] = sw0, sw1, 1.0 - sw0, 1.0 - sw1
    flags[:, 4], flags[:, 5] = f3, 1.0 - f3
    rep = lambda v: np.ascontiguousarray(np.broadcast_to(np.asarray(v, np.float32)[None, :], (128, len(v))))
    return {
        "xs": xs, "xh": xh, "w_in": w_in, "wg": wg, "gp": gp, "cvg": np.ascontiguousarray(cvg),
        "cvm": np.ascontiguousarray(cvm), "flags": flags,
        "nwr": rep(inp["norm_w"][0]),
        "gnw": rep(np.tile(inp["gdn_norm_w"][0], H)),
        "mnw": rep(inp["mlstm_norm_w"][0]),
        "gb": rep(inp["gate_bias"][0]),
        "fnw": rep(inp["final_norm_w"]),
        "wa": np.ascontiguousarray(inp["w_branch_gdn"][0]),
        "wb": np.ascontiguousarray(inp["w_branch_mlstm"][0]),
        "wo": np.ascontiguousarray(inp["w_out"][0]),
        "cst": make_consts(cfg),
    }


def kernel(**inputs):
    cfg = Cfg()
    inp = {k_: np.asarray(v, dtype=np.float32) for k_, v in inputs.items()}
    nc = build_nc(cfg)
    in_maps = [prep_core(inp, cfg, c // 4, c % 4) for c in range(N_CORES)]
    res = run_bass_kernel_spmd(nc, in_maps, core_ids=list(range(N_CORES)))
    out = np.zeros((2, 4 * cfg.SEG, cfg.D), np.float32)
    for c in range(N_CORES):
        b, tq = c // 4, c % 4
        out[b, tq * cfg.SEG:(tq + 1) * cfg.SEG] = res.results[c]["out"]
    return out
```
